# Optimizing a Trainium2 kernel written in Bass

```python
import jax, jax.numpy as jnp
from jax import lax
import numpy as np

D_MODEL = 2048
BATCH = 2
SEQ = 4096
DEPTH = 4

HEAD_DIM = 64
D_MIX = D_MODEL
GMLP_WIDTH = D_MIX // 4
ATTN_WIDTH = D_MIX // 2
FNET_WIDTH = D_MIX // 4
GMLP_HEADS = GMLP_WIDTH // HEAD_DIM
ATTN_HEADS = ATTN_WIDTH // HEAD_DIM
FNET_GROUPS = FNET_WIDTH // HEAD_DIM
CHUNK = 128
Q_BLOCK = 128
DILATED_PATTERNS = ((128, 1), (512, 4), (2048, 16))
REL_BUCKETS = 32
REL_MAX_DISTANCE = 1024
D_FF = 5632
CONV_WIDTH = 3
EPS = 1e-6
IN_WIDTH = 2 * GMLP_WIDTH + 3 * ATTN_WIDTH + FNET_WIDTH

kernel_name = "hybrid_gmlp_dilated_fnet_encoder"


def _rms_norm(x, g):
    xf = x.astype(jnp.float32)
    y = xf * lax.rsqrt(jnp.mean(xf * xf, axis=-1, keepdims=True) + EPS)
    return (y * g.astype(jnp.float32)).astype(x.dtype)


def _t5_bucket(rel):
    half = REL_BUCKETS // 2
    max_exact = half // 2
    n = np.abs(rel)
    nl = np.maximum(n, max_exact).astype(np.float32)
    large = max_exact + (np.log(nl / max_exact) / np.log(REL_MAX_DISTANCE / max_exact)
                         * (half - max_exact)).astype(np.int32)
    large = np.minimum(large, half - 1)
    b = np.where(n < max_exact, n, large) + (rel > 0).astype(np.int32) * half
    return b.astype(np.int32)


def _dilated_attention(q, k, v, rel_bias):
    bsz, seq, heads, hd = q.shape
    scale = hd ** -0.5
    patterns = []
    for window, dil in DILATED_PATTERNS:
        half = window // (2 * dil)
        offs = dil * np.arange(-half, half + 1, dtype=np.int32)
        bias = rel_bias[jnp.asarray(_t5_bucket(offs))].T.astype(jnp.float32)
        patterns.append((jnp.asarray(offs), bias))

    def block(i):
        start = i * Q_BLOCK
        qb = lax.dynamic_slice_in_dim(q, start, Q_BLOCK, axis=1)
        qpos = start + jnp.arange(Q_BLOCK, dtype=jnp.int32)
        outs, lses = [], []
        for offs, bias in patterns:
            idx = qpos[:, None] + offs[None, :]
            valid = (idx >= 0) & (idx < seq)
            idx = jnp.clip(idx, 0, seq - 1)
            kg = jnp.take(k, idx, axis=1)
            vg = jnp.take(v, idx, axis=1)
            logits = jnp.einsum('bqhd,bqkhd->bqhk', qb, kg).astype(jnp.float32) * scale
            logits = logits + bias[None, None]
            logits = jnp.where(valid[None, :, None, :], logits, -1e30)
            m = jnp.max(logits, axis=-1, keepdims=True)
            p = jnp.exp(logits - m)
            s = jnp.sum(p, axis=-1, keepdims=True)
            o = jnp.einsum('bqhk,bqkhd->bqhd', (p / s).astype(v.dtype), vg)
            outs.append(o.astype(jnp.float32))
            lses.append((m + jnp.log(s))[..., 0])
        w = jax.nn.softmax(jnp.stack(lses, axis=0), axis=0)
        out = jnp.sum(w[..., None] * jnp.stack(outs, axis=0), axis=0)
        return out.astype(q.dtype)

    ob = lax.map(block, jnp.arange(seq // Q_BLOCK))
    return jnp.transpose(ob, (1, 0, 2, 3, 4)).reshape(bsz, seq, heads * hd)


def _hybrid_layer(x, w_in, gmlp_ws, gmlp_b, fnet_w, mix_gain, w_out, norm_mix,
                  norm_ffn, ffn_up, ffn_conv_w, ffn_conv_b, ffn_down, rel_bias):
    bsz, seq, _ = x.shape
    xn = _rms_norm(x, norm_mix)
    z = xn @ w_in
    o1 = 2 * GMLP_WIDTH
    o2 = o1 + 3 * ATTN_WIDTH
    za, zq, zc = z[..., :o1], z[..., o1:o2], z[..., o2:]

    za = jax.nn.gelu(za, approximate=False)
    u, vg = za[..., :GMLP_WIDTH], za[..., GMLP_WIDTH:]
    vg = vg.reshape(bsz, seq // CHUNK, CHUNK, GMLP_HEADS, HEAD_DIM)
    gate = jnp.einsum('hij,bcjhd->bcihd', gmlp_ws, vg) + gmlp_b.T[None, None, :, :, None]
    a_out = u * gate.reshape(bsz, seq, GMLP_WIDTH)

    q = zq[..., :ATTN_WIDTH].reshape(bsz, seq, ATTN_HEADS, HEAD_DIM)
    k = zq[..., ATTN_WIDTH:2 * ATTN_WIDTH].reshape(bsz, seq, ATTN_HEADS, HEAD_DIM)
    v = zq[..., 2 * ATTN_WIDTH:].reshape(bsz, seq, ATTN_HEADS, HEAD_DIM)
    b_out = _dilated_attention(q, k, v, rel_bias)

    zc = zc.reshape(bsz, seq, FNET_GROUPS, HEAD_DIM).astype(jnp.float32)
    f = jnp.fft.fft2(zc, axes=(1, 3), norm='ortho').real.astype(x.dtype)
    c_out = jnp.einsum('bsgc,gce->bsge', f, fnet_w).reshape(bsz, seq, FNET_WIDTH)

    ga = mix_gain[:GMLP_WIDTH]
    gb = mix_gain[GMLP_WIDTH:GMLP_WIDTH + ATTN_WIDTH]
    gc = mix_gain[GMLP_WIDTH + ATTN_WIDTH:]
    mixed = jnp.concatenate([_rms_norm(a_out, ga), _rms_norm(b_out, gb), _rms_norm(c_out, gc)], axis=-1)
    x = x + mixed @ w_out

    hn = _rms_norm(x, norm_ffn)
    h = hn @ ffn_up
    hp = jnp.pad(h, ((0, 0), (1, 1), (0, 0)))
    h = hp[:, :-2] * ffn_conv_w[0] + hp[:, 1:-1] * ffn_conv_w[1] + hp[:, 2:] * ffn_conv_w[2] + ffn_conv_b
    g, up = h[..., :D_FF], h[..., D_FF:]
    x = x + (jax.nn.silu(g) * up) @ ffn_down
    return x


def setup_inputs(seed: int = 0) -> dict:
    key = jax.random.key(seed)
    ks = jax.random.split(key, 16)
    f32 = jnp.float32
    nrm = lambda k, shape, s: jax.random.normal(k, shape, f32) * s
    return {
        'x': nrm(ks[0], (BATCH, SEQ, D_MODEL), 1.0),
        'w_in': nrm(ks[1], (DEPTH, D_MODEL, IN_WIDTH), D_MODEL ** -0.5),
        'gmlp_ws': nrm(ks[2], (DEPTH, GMLP_HEADS, CHUNK, CHUNK), CHUNK ** -0.5),
        'gmlp_b': 1.0 + nrm(ks[3], (DEPTH, GMLP_HEADS, CHUNK), 0.01),
        'fnet_w': nrm(ks[4], (DEPTH, FNET_GROUPS, HEAD_DIM, HEAD_DIM), HEAD_DIM ** -0.5),
        'mix_gain': 1.0 + nrm(ks[5], (DEPTH, D_MIX), 0.01),
        'w_out': nrm(ks[6], (DEPTH, D_MIX, D_MODEL), D_MIX ** -0.5),
        'norm_mix': 1.0 + nrm(ks[7], (DEPTH, D_MODEL), 0.01),
        'norm_ffn': 1.0 + nrm(ks[8], (DEPTH, D_MODEL), 0.01),
        'ffn_up': nrm(ks[9], (DEPTH, D_MODEL, 2 * D_FF), D_MODEL ** -0.5),
        'ffn_conv_w': nrm(ks[10], (DEPTH, CONV_WIDTH, 2 * D_FF), CONV_WIDTH ** -0.5),
        'ffn_conv_b': nrm(ks[11], (DEPTH, 2 * D_FF), 0.01),
        'ffn_down': nrm(ks[12], (DEPTH, D_FF, D_MODEL), D_FF ** -0.5),
        'rel_bias': nrm(ks[13], (REL_BUCKETS, ATTN_HEADS), 0.5),
        'final_norm': 1.0 + nrm(ks[14], (D_MODEL,), 0.01),
    }


def reference(x, w_in, gmlp_ws, gmlp_b, fnet_w, mix_gain, w_out, norm_mix, norm_ffn,
              ffn_up, ffn_conv_w, ffn_conv_b, ffn_down, rel_bias, final_norm):
    for l in range(DEPTH):
        x = _hybrid_layer(x, w_in[l], gmlp_ws[l], gmlp_b[l], fnet_w[l], mix_gain[l], w_out[l],
                          norm_mix[l], norm_ffn[l], ffn_up[l], ffn_conv_w[l], ffn_conv_b[l],
                          ffn_down[l], rel_bias)
    return _rms_norm(x, final_norm)
```

```python
from contextlib import ExitStack
import numpy as np
import ml_dtypes
import concourse.bass as bass
import concourse.mybir as mybir
from concourse.bass_utils import run_bass_kernel_spmd

F32 = mybir.dt.float32
BF16 = mybir.dt.bfloat16
AF = mybir.ActivationFunctionType
ALU = mybir.AluOpType
AX = mybir.AxisListType
NPBF = ml_dtypes.bfloat16

D = 2048
SEQ = 4096
NB = 2
DEPTH = 4
NCORE = 8
TOK = 1024
KC = D // 128
INW = 4608
DFF = 5632
NF = DFF // 128
EPS = 1e-6
PATTERNS = ((128, 1), (512, 4), (2048, 16))

SEM_LIMIT = 30000
N_DMA_SEMS = 24


class Dep:
    __slots__ = ("name", "writer", "readers")

    def __init__(self, name=""):
        self.name = name
        self.writer = None
        self.readers = {}


class KB:
    ENGS = ("pe", "act", "dve", "pool", "sp")

    def __init__(self, nc):
        self.nc = nc
        self.es = ExitStack()
        self.q = {e: [] for e in self.ENGS}
        self.cnt = {e: 0 for e in self.ENGS}
        self.epoch = {e: 0 for e in self.ENGS}
        self.sems = {}
        self.waited = {e: {} for e in self.ENGS}
        self.dma_slots = {}
        self.dma_next = {}
        self.ndep = 0
        self.n_inst = 0
        self.out_evs = []

    def sb(self, name, shape, dtype):
        return self.es.enter_context(self.nc.sbuf_tensor(name, list(shape), dtype))

    def ps(self, name, shape, dtype=F32):
        return self.es.enter_context(self.nc.psum_tensor(name, list(shape), dtype))

    def dep(self, name=""):
        self.ndep += 1
        return Dep(name or f"d{self.ndep}")

    def deps(self, n, name=""):
        return [self.dep(f"{name}{i}") for i in range(n)]

    def _sem(self, key):
        if key not in self.sems:
            self.sems[key] = self.es.enter_context(
                self.nc.semaphore(f"s_{key[0]}_{key[1]}"))
        return self.sems[key]

    def _wait(self, eng, ev):
        key, val = ev
        w = self.waited[eng]
        if w.get(key, 0) >= val:
            return
        w[key] = val
        sem = self._sem(key)
        self.q[eng].append(lambda e, sem=sem, val=val: e.wait_ge(sem, val))

    def _collect(self, eng, reads, writes, skip_same_engine=False):
        evs = []
        for d in reads:
            if d.writer is not None:
                evs.append(d.writer)
        for d in writes:
            if d.writer is not None:
                evs.append(d.writer)
            evs.extend(d.readers.values())
        for ev in evs:
            if skip_same_engine and ev[0][0] == eng:
                continue
            self._wait(eng, ev)

    def _record(self, ev, rkey, reads, writes):
        for d in reads:
            d.readers[rkey] = ev
        for d in writes:
            d.writer = ev
            d.readers = {}

    def op(self, eng, fn, reads=(), writes=(), signal=True):
        self._collect(eng, reads, writes, skip_same_engine=(eng == "pe"))
        self.n_inst += 1
        if not signal:
            if self.cnt[eng] >= SEM_LIMIT:
                ev = ((eng, self.epoch[eng] + 1), 1)
            else:
                ev = ((eng, self.epoch[eng]), self.cnt[eng] + 1)
            self.q[eng].append(lambda e, fn=fn: fn(e))
            self._record(ev, eng, reads, writes)
            return None
        if self.cnt[eng] >= SEM_LIMIT:
            self.epoch[eng] += 1
            self.cnt[eng] = 0
        self.cnt[eng] += 1
        key = (eng, self.epoch[eng])
        sem = self._sem(key)
        ev = (key, self.cnt[eng])
        self.q[eng].append(lambda e, fn=fn, sem=sem: fn(e).then_inc(sem, 1))
        self._record(ev, eng, reads, writes)
        return ev

    def dma(self, queue, out_ap, in_ap, reads=(), writes=(), **kw):
        if queue not in self.dma_slots:
            self.dma_slots[queue] = [0] * N_DMA_SEMS
            self.dma_next[queue] = 0
        i = self.dma_next[queue]
        self.dma_next[queue] = (i + 1) % N_DMA_SEMS
        key = ("dma_" + queue, i)
        prev = self.dma_slots[queue][i]
        if prev:
            self._wait(queue, (key, prev))
        self._collect(queue, reads, writes)
        val = prev + 16
        self.dma_slots[queue][i] = val
        sem = self._sem(key)
        self.n_inst += 1
        self.q[queue].append(
            lambda e, o=out_ap, a=in_ap, sem=sem, kw=kw:
            e.dma_start(out=o, in_=a, **kw).then_inc(sem, 16))
        ev = (key, val)
        self._record(ev, key, reads, writes)
        return ev

    def out_dma(self, queue, out_ap, in_ap, reads=()):
        ev = self.dma(queue, out_ap, in_ap, reads=reads)
        self.out_evs.append(ev)
        return ev

    def emit(self):
        for ev in self.out_evs:
            self._wait("sp", ev)
        nc = self.nc
        q = self.q
        with nc.Block() as block:
            @block.sync
            def _(e):
                for f in q["sp"]:
                    f(e)

            @block.tensor
            def _(e):
                for f in q["pe"]:
                    f(e)

            @block.scalar
            def _(e):
                for f in q["act"]:
                    f(e)

            @block.vector
            def _(e):
                for f in q["dve"]:
                    f(e)

            @block.gpsimd
            def _(e):
                for f in q["pool"]:
                    f(e)
        self.es.close()


class Rot:
    def __init__(self, k, name, n, shape, dtype, psum=False):
        self.bufs = []
        for i in range(n):
            t = k.ps(f"{name}{i}", shape, dtype) if psum else k.sb(f"{name}{i}", shape, dtype)
            self.bufs.append((t, k.dep(f"{name}{i}")))
        self.i = 0

    def next(self):
        b = self.bufs[self.i]
        self.i = (self.i + 1) % len(self.bufs)
        return b


def rms_stats(k, src_fn, nchunk, ntok_tiles, ones_bf, d_ones, sq_rot, ps_rot, rstd, d_rstd,
              inv_n, src_deps):
    for (t0, tn) in ntok_tiles:
        pt, dpt = ps_rot.next()
        for kc in range(nchunk):
            sq, dsq = sq_rot.next()
            k.op("act", lambda e, sq=sq, kc=kc, t0=t0, tn=tn: e.activation(
                sq[:, 0:tn], src_fn(kc, slice(t0, t0 + tn)), AF.Square),
                reads=src_deps, writes=[dsq])
            k.op("pe", lambda e, pt=pt, sq=sq, kc=kc, tn=tn: e.matmul(
                pt[:, 0:tn], ones_bf[:], sq[:, 0:tn], start=(kc == 0), stop=(kc == nchunk - 1)),
                reads=[dsq, d_ones], writes=[dpt], signal=True)
        k.op("dve", lambda e, pt=pt, t0=t0, tn=tn: e.tensor_scalar(
            rstd[:, t0:t0 + tn], pt[:, 0:tn], inv_n, EPS, ALU.mult, ALU.add),
            reads=[dpt], writes=[d_rstd])
        k.op("act", lambda e, t0=t0, tn=tn: e.activation(
            rstd[:, t0:t0 + tn], rstd[:, t0:t0 + tn], AF.Sqrt),
            reads=[d_rstd], writes=[d_rstd])
        k.op("dve", lambda e, t0=t0, tn=tn: e.reciprocal(
            rstd[:, t0:t0 + tn], rstd[:, t0:t0 + tn]),
            reads=[d_rstd], writes=[d_rstd])


def build_A(NT=4):
    nc = bass.Bass("TRN2", target_bir_lowering=False)
    xT = nc.dram_tensor("xT", [NT, D, TOK], F32, kind="ExternalInput").ap()
    w_in = nc.dram_tensor("w_in", [D, INW], F32, kind="ExternalInput").ap()
    gn = nc.dram_tensor("gn", [128, KC], F32, kind="ExternalInput").ap()
    wsT = nc.dram_tensor("wsT", [128, 8, 128], F32, kind="ExternalInput").ap()
    gb = nc.dram_tensor("gb", [128, 4, 512], F32, kind="ExternalInput").ap()
    o_qT = nc.dram_tensor("o_qT", [NT] + [1024, TOK], BF16, kind="ExternalOutput").ap()
    o_kT = nc.dram_tensor("o_kT", [NT] + [1024, TOK], BF16, kind="ExternalOutput").ap()
    o_va = nc.dram_tensor("o_va", [NT] + [TOK, 1024], BF16, kind="ExternalOutput").ap()
    o_zc = nc.dram_tensor("o_zc", [NT] + [TOK, 512], BF16, kind="ExternalOutput").ap()
    o_aT = nc.dram_tensor("o_aT", [NT] + [512, TOK], F32, kind="ExternalOutput").ap()

    k = KB(nc)
    x_sb = k.sb("x_sb", [128, KC, TOK], F32)
    xn = k.sb("xn", [128, KC, TOK], BF16)
    g_sb = k.sb("g_sb", [128, KC], F32)
    ws_sb = k.sb("ws_sb", [128, 8, 128], BF16)
    gb_sb = k.sb("gb_sb", [128, 4, 512], F32)
    ones_bf = k.sb("ones_bf", [128, 128], BF16)
    rstd = k.sb("rstd", [128, TOK], F32)
    uT = k.sb("uT", [128, 4, TOK], F32)
    vtok = k.sb("vtok", [128, 8, 512], BF16)
    d_x = k.deps(KC, "x")
    d_xn, d_g, d_ws, d_gb, d_ones, d_rstd = (k.dep(n) for n in
                                              ("xn", "g", "ws", "gb", "ones", "rstd"))
    d_u = k.deps(4, "u")
    d_v = k.deps(8, "v")
    d_a = k.deps(4, "a")
    wrot = Rot(k, "wblk", 3, [128, KC, 512], BF16)
    sqrot = Rot(k, "sq", 3, [128, 512], BF16)
    psrot = Rot(k, "ps", 4, [128, 512], F32, psum=True)
    gps = Rot(k, "gps", 4, [128, 512], F32, psum=True)
    stf = Rot(k, "stf", 3, [128, TOK], BF16)
    stt = Rot(k, "stt", 3, [128, 512], BF16)
    gtmp = Rot(k, "gtmp", 2, [128, 512], F32)

    k.dma("sp", g_sb[:], gn, writes=[d_g])
    k.dma("sp", gb_sb[:], gb, writes=[d_gb])
    k.dma("pool", ws_sb[:], wsT, writes=[d_ws])
    k.op("dve", lambda e: e.memset(ones_bf[:], 1.0), writes=[d_ones])

    for tl in range(NT):
        for kc in range(KC):
            k.dma("sp", x_sb[:, kc, :], xT[tl, kc * 128:(kc + 1) * 128, :], writes=[d_x[kc]])
        NBLK = INW // 512
        wtiles = {}

        def load_w(b):
            t, dt_ = wrot.next()
            k.dma("pool", t[:], w_in[:, b * 512:(b + 1) * 512].rearrange("(kc p) n -> p kc n", p=128),
                  writes=[dt_])
            wtiles[b] = (t, dt_)

        load_w(0)
        load_w(1)

        rms_stats(k, lambda kc, sl: x_sb[:, kc, sl], KC, [(0, 512), (512, 512)], ones_bf, d_ones,
                  sqrot, psrot, rstd, d_rstd, 1.0 / D, d_x)
        for kc in range(KC):
            for h in range(2):
                sl = slice(h * 512, (h + 1) * 512)
                k.op("dve", lambda e, kc=kc, sl=sl: e.scalar_tensor_tensor(
                    xn[:, kc, sl], x_sb[:, kc, sl], g_sb[:, kc:kc + 1], rstd[:, sl],
                    ALU.mult, ALU.mult),
                    reads=[d_x[kc], d_g, d_rstd], writes=[d_xn])

        ei = [0]

        def evac_copy(out_ap, in_ap, reads, writes, scale=None):
            ei[0] += 1
            if ei[0] % 2 == 0:
                if scale is None:
                    k.op("act", lambda e: e.copy(out_ap, in_ap), reads=reads, writes=writes)
                else:
                    k.op("act", lambda e: e.mul(out_ap, in_ap, scale), reads=reads, writes=writes)
            else:
                if scale is None:
                    k.op("dve", lambda e: e.tensor_copy(out_ap, in_ap), reads=reads, writes=writes)
                else:
                    k.op("dve", lambda e: e.tensor_scalar(out_ap, in_ap, scale, None, ALU.mult),
                         reads=reads, writes=writes)

        for b in range(NBLK):
            if b + 2 < NBLK:
                load_w(b + 2)
            wt, dwt = wtiles.pop(b)
            feat_major = b in (0, 2, 3, 4, 5)
            if feat_major:
                for m in range(4):
                    if b != 0:
                        st, dst = stf.next()
                    for nt in range(2):
                        pt, dpt = psrot.next()
                        for kc in range(KC):
                            k.op("pe", lambda e, pt=pt, wt=wt, m=m, kc=kc, nt=nt: e.matmul(
                                pt[:], wt[:, kc, m * 128:(m + 1) * 128],
                                xn[:, kc, nt * 512:(nt + 1) * 512],
                                start=(kc == 0), stop=(kc == KC - 1)),
                                reads=[dwt, d_xn], writes=[dpt], signal=(kc == KC - 1))
                        sl = slice(nt * 512, (nt + 1) * 512)
                        if b == 0:
                            k.op("act", lambda e, pt=pt, m=m, sl=sl: e.activation(
                                uT[:, m, sl], pt[:], AF.Gelu), reads=[dpt], writes=[d_u[m]])
                        else:
                            evac_copy(st[:, sl], pt[:], [dpt], [dst],
                                      scale=(0.125 if b in (2, 3) else None))
                    if b != 0:
                        row0 = ((b - 2) % 2) * 512 + m * 128
                        dst_t = o_qT if b in (2, 3) else o_kT
                        k.out_dma("sp", dst_t[tl, row0:row0 + 128, :], st[:], reads=[dst])
            else:
                for tt in range(8):
                    pt, dpt = psrot.next()
                    for kc in range(KC):
                        k.op("pe", lambda e, pt=pt, wt=wt, tt=tt, kc=kc: e.matmul(
                            pt[:], xn[:, kc, tt * 128:(tt + 1) * 128], wt[:, kc, :],
                            start=(kc == 0), stop=(kc == KC - 1)),
                            reads=[dwt, d_xn], writes=[dpt], signal=(kc == KC - 1))
                    if b == 1:
                        k.op("act", lambda e, pt=pt, tt=tt: e.activation(
                            vtok[:, tt, :], pt[:], AF.Gelu), reads=[dpt], writes=[d_v[tt]])
                    else:
                        st, dst = stt.next()
                        evac_copy(st[:], pt[:], [dpt], [dst])
                        if b in (6, 7):
                            k.out_dma("sp", o_va[tl, tt * 128:(tt + 1) * 128, (b - 6) * 512:(b - 5) * 512],
                                      st[:], reads=[dst])
                        else:
                            k.out_dma("sp", o_zc[tl, tt * 128:(tt + 1) * 128, :], st[:], reads=[dst])
            if b == 1:
                for pr in range(4):
                    for nt in range(2):
                        pa, dpa = gps.next()
                        pb, dpb = gps.next()
                        for c4 in range(4):
                            tt = nt * 4 + c4
                            for (pp, dpp, h) in ((pa, dpa, 2 * pr), (pb, dpb, 2 * pr + 1)):
                                k.op("pe", lambda e, pp=pp, tt=tt, pr=pr, h=h, c4=c4: e.matmul(
                                    pp[:, c4 * 128:(c4 + 1) * 128],
                                    vtok[:, tt, pr * 128:(pr + 1) * 128], ws_sb[:, h, :],
                                    start=True, stop=True),
                                    reads=[d_v[tt], d_ws], writes=[dpp], signal=(c4 == 3))
                        tmp, dtmp = gtmp.next()
                        sl = slice(nt * 512, (nt + 1) * 512)
                        for (pp, dpp, lo) in ((pa, dpa, 0), (pb, dpb, 64)):
                            k.op("dve", lambda e, pp=pp, lo=lo, tmp=tmp, pr=pr: e.tensor_tensor(
                                tmp[lo:lo + 64, :], pp[lo:lo + 64, :], gb_sb[lo:lo + 64, pr, :], ALU.add),
                                reads=[dpp, d_gb], writes=[dtmp])
                        k.op("dve", lambda e, tmp=tmp, pr=pr, sl=sl: e.tensor_tensor(
                            uT[:, pr, sl], tmp[:], uT[:, pr, sl], ALU.mult),
                            reads=[dtmp, d_u[pr]], writes=[d_u[pr]])
                    k.out_dma("sp", o_aT[tl, pr * 128:(pr + 1) * 128, :], uT[:, pr, :], reads=[d_u[pr]])
    k.emit()
    return nc


def pat_geom(d):
    L = SEQ // d
    return L, d * (L + 128)


def build_B1():
    nc = bass.Bass("TRN2", target_bir_lowering=False)
    ins = {}
    for pp in range(2):
        for p, (_, d) in enumerate(PATTERNS):
            L, PL = pat_geom(d)
            ins[("k", pp, p)] = nc.dram_tensor(f"k_{pp}_{p}", [128, PL], BF16, kind="ExternalInput").ap()
            ins[("qa", pp, p)] = nc.dram_tensor(f"qa_{pp}_{p}", [128, SEQ], BF16, kind="ExternalInput").ap()
            ins[("qb", pp, p)] = nc.dram_tensor(f"qb_{pp}_{p}", [128, SEQ], BF16, kind="ExternalInput").ap()
            ins[("v", pp, p)] = nc.dram_tensor(f"v_{pp}_{p}", [128, 2, PL // 128, 65], BF16,
                                               kind="ExternalInput").ap()
    bias = nc.dram_tensor("bias", [128, 12, 512], F32, kind="ExternalInput").ap()
    selin = nc.dram_tensor("sel", [128, 128], F32, kind="ExternalInput").ap()
    o_bT = nc.dram_tensor("o_bT", [256, SEQ], F32, kind="ExternalOutput").ap()

    k = KB(nc)
    PLMAX = pat_geom(16)[1]
    krot = Rot(k, "k_sb", 2, [128, PLMAX], BF16)
    qarot = Rot(k, "qa_sb", 2, [128, SEQ], BF16)
    qbrot = Rot(k, "qb_sb", 2, [128, SEQ], BF16)
    vrot = Rot(k, "v_sb", 2, [128, 2, PLMAX // 128, 65], BF16)
    E = k.sb("E", [128, 12, 512], F32)
    d_E = k.dep("E")
    sel = k.sb("sel_sb", [128, 128], F32)
    d_sel = k.dep("sel")
    acc = [k.sb(f"acc{i}", [128, SEQ], F32) for i in range(2)]
    d_acc = k.deps(2, "acc")
    srot = Rot(k, "S", 3, [128, 512], F32, psum=True)
    orot = Rot(k, "O", 3, [128, 256], F32, psum=True)
    bcrot = Rot(k, "bc", 2, [128, 512], F32, psum=True)
    pfrot = Rot(k, "Pf", 3, [128, 512], F32)
    pbrot = Rot(k, "Pb", 3, [128, 512], BF16)
    recrot = Rot(k, "rec", 2, [64, 512], F32)
    strot = Rot(k, "st", 2, [64, 512], F32)

    k.dma("sp", E[:], bias, writes=[d_E])
    k.dma("sp", sel[:], selin, writes=[d_sel])
    for i in range(12):
        k.op("act", lambda e, i=i: e.activation(E[:, i, :], E[:, i, :], AF.Exp),
             reads=[d_E], writes=[d_E])

    loaded = {}

    def load(pp, p):
        d = PATTERNS[p][1]
        L, PL = pat_geom(d)
        kt, dk = krot.next()
        qa, dqa = qarot.next()
        qb, dqb = qbrot.next()
        vt, dv = vrot.next()
        k.dma("sp", kt[:, 0:PL], ins[("k", pp, p)], writes=[dk])
        k.dma("sp", qa[:], ins[("qa", pp, p)], writes=[dqa])
        k.dma("sp", qb[:], ins[("qb", pp, p)], writes=[dqb])
        k.dma("sp", vt[:, :, 0:PL // 128, :], ins[("v", pp, p)], writes=[dv])
        loaded[(pp, p)] = (kt, dk, qa, dqa, qb, dqb, vt, dv)

    order = [(pp, p) for pp in range(2) for p in range(3)]
    load(*order[0])
    for oi, (pp, p) in enumerate(order):
        if oi + 1 < len(order):
            load(*order[oi + 1])
        d = PATTERNS[p][1]
        L, PL = pat_geom(d)
        kt, dk, qa, dqa, qb, dqb, vt, dv = loaded.pop((pp, p))
        nblk = L // 128
        for hl in range(2):
            q_sb, dq = (qa, dqa) if hl == 0 else (qb, dqb)
            ei = (pp * 2 + hl) * 3 + p
            for r in range(d):
                for bp in range(nblk // 2):
                    S, dS = srot.next()
                    for blk in range(2):
                        m0 = (bp * 2 + blk) * 128
                        qcol = r * L + m0
                        kbase = r * (L + 128) + m0
                        for j in range(2):
                            c0 = (blk * 2 + j) * 128
                            k.op("pe", lambda e, S=S, c0=c0, kt=kt, kb=kbase + 128 * j, q_sb=q_sb, qcol=qcol:
                                 e.matmul(S[:, c0:c0 + 128], kt[:, kb:kb + 128], q_sb[:, qcol:qcol + 128],
                                          start=True, stop=True),
                                 reads=[dk, dq], writes=[dS], signal=(blk == 1 and j == 1))
                    Pf, dPf = pfrot.next()
                    k.op("act", lambda e, Pf=Pf, S=S: e.activation(Pf[:], S[:], AF.Exp),
                         reads=[dS], writes=[dPf])
                    Pb, dPb = pbrot.next()
                    k.op("dve", lambda e, Pb=Pb, Pf=Pf, ei=ei: e.tensor_tensor(
                        Pb[:], Pf[:], E[:, ei, :], ALU.mult), reads=[dPf, d_E], writes=[dPb])
                    if bp == 0:
                        k.op("dve", lambda e, Pb=Pb: e.memset(Pb[0:64, 0:128], 0.0), writes=[dPb])
                    if bp == nblk // 2 - 1:
                        k.op("dve", lambda e, Pb=Pb: e.memset(Pb[64:128, 384:512], 0.0), writes=[dPb])
                    O, dO = orot.next()
                    for blk in range(2):
                        m0 = (bp * 2 + blk) * 128
                        kti = (r * (L + 128) + m0) // 128
                        for j in range(2):
                            c0 = (blk * 2 + j) * 128
                            k.op("pe", lambda e, O=O, blk=blk, vt=vt, hl=hl, ti=kti + j, Pb=Pb, c0=c0, j=j:
                                 e.matmul(O[0:65, blk * 128:(blk + 1) * 128], vt[:, hl, ti, :],
                                          Pb[:, c0:c0 + 128], start=(j == 0), stop=(j == 1)),
                                 reads=[dv, dPb], writes=[dO], signal=(blk == 1 and j == 1))
                    s0 = r + d * bp * 256
                    av = acc[hl][0:65, s0:s0 + 255 * d + 1:d] if d > 1 else acc[hl][0:65, s0:s0 + 256]
                    if p == 0:
                        k.op("dve", lambda e, av=av, O=O: e.tensor_copy(av, O[0:65, :]),
                             reads=[dO], writes=[d_acc[hl]])
                    else:
                        k.op("dve", lambda e, av=av, O=O: e.tensor_tensor(av, av, O[0:65, :], ALU.add),
                             reads=[dO, d_acc[hl]], writes=[d_acc[hl]])
        if p == 2:
            for hl in range(2):
                hrow = (pp * 2 + hl) * 64
                for nt in range(SEQ // 512):
                    sl = slice(nt * 512, (nt + 1) * 512)
                    bc, dbc = bcrot.next()
                    k.op("pe", lambda e, bc=bc, hl=hl, sl=sl: e.matmul(
                        bc[:], sel[0:65, :], acc[hl][0:65, sl], start=True, stop=True),
                        reads=[d_sel, d_acc[hl]], writes=[dbc])
                    rec, drec = recrot.next()
                    k.op("dve", lambda e, rec=rec, bc=bc: e.reciprocal(rec[:], bc[0:64, :]),
                         reads=[dbc], writes=[drec])
                    st, dst = strot.next()
                    k.op("dve", lambda e, st=st, rec=rec, hl=hl, sl=sl: e.tensor_tensor(
                        st[:], acc[hl][0:64, sl], rec[:], ALU.mult),
                        reads=[drec, d_acc[hl]], writes=[dst])
                    k.out_dma("pool", o_bT[hrow:hrow + 64, sl], st[:], reads=[dst])
    k.emit()
    return nc


def build_B2():
    nc = bass.Bass("TRN2", target_bir_lowering=False)
    zc = nc.dram_tensor("zc", [NB * 4, 128, 32, 128], BF16, kind="ExternalInput").ap()
    ct = nc.dram_tensor("ct", [128, 32, 512], BF16, kind="ExternalInput").ap()
    stb = nc.dram_tensor("st", [128, 32, 512], BF16, kind="ExternalInput").ap()
    bdc = nc.dram_tensor("bdc", [128, 128], BF16, kind="ExternalInput").ap()
    bds = nc.dram_tensor("bds", [128, 128], BF16, kind="ExternalInput").ap()
    bdw = nc.dram_tensor("bdw", [128, 4, 128], F32, kind="ExternalInput").ap()
    o_cT = nc.dram_tensor("o_cT", [NB * 4, 128, 512], F32, kind="ExternalOutput").ap()

    k = KB(nc)
    Ct = k.sb("Ct", [128, 32, 512], BF16)
    St = k.sb("St", [128, 32, 512], BF16)
    BDC = k.sb("BDC", [128, 128], BF16)
    BDS = k.sb("BDS", [128, 128], BF16)
    BDW = k.sb("BDW", [128, 4, 128], BF16)
    d_ct, d_st, d_bd = k.dep("ct"), k.dep("st"), k.dep("bd")
    zrot = Rot(k, "z", 2, [128, 32, 128], BF16)
    prot = Rot(k, "pp", 6, [128, 512], F32, psum=True)
    pcrot = Rot(k, "pc", 2, [128, 512], BF16)
    psrot = Rot(k, "psb", 2, [128, 512], BF16)
    frot = Rot(k, "f", 2, [128, 512], BF16)
    orot = Rot(k, "o", 2, [128, 512], F32)

    k.dma("sp", BDC[:], bdc, writes=[d_bd])
    k.dma("sp", BDS[:], bds, writes=[d_bd])
    k.dma("pool", BDW[:], bdw, writes=[d_bd])
    for t4 in range(4):
        k.dma("sp", Ct[:, t4 * 8:(t4 + 1) * 8, :], ct[:, t4 * 8:(t4 + 1) * 8, :], writes=[d_ct])
    for t4 in range(4):
        k.dma("sp", St[:, t4 * 8:(t4 + 1) * 8, :], stb[:, t4 * 8:(t4 + 1) * 8, :], writes=[d_st])
    ztiles = {}

    def loadz(u):
        zt, dz = zrot.next()
        k.dma("sp", zt[:], zc[u], writes=[dz])
        ztiles[u] = (zt, dz)

    loadz(0)
    for u in range(NB * 4):
        if u + 1 < NB * 4:
            loadz(u + 1)
        zt, dz = ztiles.pop(u)
        gp = u % 4
        pc, dpc = prot.next()
        for t in range(32):
            k.op("pe", lambda e, pc=pc, zt=zt, t=t: e.matmul(pc[:], zt[:, t, :], Ct[:, t, :],
                                                           start=(t == 0), stop=(t == 31)),
                 reads=[dz, d_ct], writes=[dpc], signal=(t == 31))
        ps_, dps = prot.next()
        for t in range(32):
            k.op("pe", lambda e, ps_=ps_, zt=zt, t=t: e.matmul(ps_[:], zt[:, t, :], St[:, t, :],
                                                             start=(t == 0), stop=(t == 31)),
                 reads=[dz, d_st], writes=[dps], signal=(t == 31))
        pcb, dpcb = pcrot.next()
        k.op("act", lambda e, pcb=pcb, pc=pc: e.copy(pcb[:], pc[:]), reads=[dpc], writes=[dpcb])
        psb, dpsb = psrot.next()
        k.op("dve", lambda e, psb=psb, ps_=ps_: e.tensor_copy(psb[:], ps_[:]), reads=[dps], writes=[dpsb])
        pf, dpf = prot.next()
        k.op("pe", lambda e, pf=pf, pcb=pcb: e.matmul(pf[:], BDC[:], pcb[:], start=True, stop=False),
             reads=[d_bd, dpcb], writes=[dpf], signal=False)
        k.op("pe", lambda e, pf=pf, psb=psb: e.matmul(pf[:], BDS[:], psb[:], start=False, stop=True),
             reads=[d_bd, dpsb], writes=[dpf])
        fb, dfb = frot.next()
        k.op("act", lambda e, fb=fb, pf=pf: e.mul(fb[:], pf[:], 1.0 / 512.0), reads=[dpf], writes=[dfb])
        po, dpo = prot.next()
        k.op("pe", lambda e, po=po, fb=fb, gp=gp: e.matmul(po[:], BDW[:, gp, :], fb[:], start=True, stop=True),
             reads=[d_bd, dfb], writes=[dpo])
        ot, dot_ = orot.next()
        k.op("dve", lambda e, ot=ot, po=po: e.tensor_copy(ot[:], po[:]), reads=[dpo], writes=[dot_])
        k.out_dma("pool", o_cT[u], ot[:], reads=[dot_])
    k.emit()
    return nc


TH = TOK + 2
TT3 = ((0, 342), (342, 342), (684, 342))
FG = 4
NG = NF // FG


def build_C(NT=4):
    final = False
    nc = bass.Bass("TRN2", target_bir_lowering=False)
    xTh = nc.dram_tensor("xTh", [NT, D, TH], F32, kind="ExternalInput").ap()
    mixTh = nc.dram_tensor("mixTh", [NT, D, TH], F32, kind="ExternalInput").ap()
    w_out = nc.dram_tensor("w_out", [D, D], F32, kind="ExternalInput").ap()
    ffn_up = nc.dram_tensor("ffn_up", [D, 2 * DFF], F32, kind="ExternalInput").ap()
    ffn_down = nc.dram_tensor("ffn_down", [DFF, D], F32, kind="ExternalInput").ap()
    gains = nc.dram_tensor("gains", [128, 3, KC], F32, kind="ExternalInput").ap()
    convw = nc.dram_tensor("convw", [128, 2 * NF, 4], F32, kind="ExternalInput").ap()
    o_xT = nc.dram_tensor("o_xT", [NT, D, TOK], F32, kind="ExternalOutput").ap()

    k = KB(nc)
    x_sb = k.sb("x_sb", [128, KC, TH], F32)
    mh = k.sb("mh", [128, KC, TH], BF16)
    g_sb = k.sb("g_sb", [128, 3, KC], F32)
    cw = k.sb("cw", [128, 2 * NF, 4], F32)
    ones_bf = k.sb("ones_bf", [128, 128], BF16)
    rstd = k.sb("rstd", [128, TH], F32)
    act = k.sb("act", [128, FG, TOK], BF16)
    dwn = k.sb("dwn", [128, FG, D], BF16)
    d_x = k.deps(KC, "x")
    d_mh, d_g, d_cw, d_ones, d_rstd, d_act, d_dwn = (
        k.dep(n) for n in ("mh", "g", "cw", "ones", "rstd", "act", "dwn"))
    mixrot = Rot(k, "mixs", 2, [128, TH], F32)
    sqrot = Rot(k, "sq", 3, [128, 512], BF16)
    wrot = Rot(k, "wblk", 4, [128, KC, 256], BF16)
    hrot = Rot(k, "h", 3, [128, TH], F32)
    cgrot = Rot(k, "cg", 2, [128, TOK], F32)
    curot = Rot(k, "cu", 1, [128, TOK], F32)
    pb = Rot(k, "pb", 6, [128, 512], F32, psum=True)
    py = Rot(k, "py", 2, [128, 512], F32, psum=True)

    k.dma("sp", g_sb[:], gains, writes=[d_g])
    k.dma("sp", cw[:], convw, writes=[d_cw])
    k.op("dve", lambda e: e.memset(ones_bf[:], 1.0), writes=[d_ones])

    def stats(chunks, src_of, inv_n, tiles):
        banks = [pb.next() for _ in tiles]
        for ci, kc in enumerate(chunks):
            fn, sdeps = src_of(kc)
            for ti, (t0, tn) in enumerate(tiles):
                sq, dsq = sqrot.next()
                k.op("act", lambda e, sq=sq, fn=fn, t0=t0, tn=tn: e.activation(
                    sq[:, 0:tn], fn(slice(t0, t0 + tn)), AF.Square), reads=sdeps, writes=[dsq])
                pt, dpt = banks[ti]
                k.op("pe", lambda e, pt=pt, sq=sq, tn=tn, ci=ci: e.matmul(
                    pt[:, 0:tn], ones_bf[:], sq[:, 0:tn], start=(ci == 0), stop=(ci == len(chunks) - 1)),
                    reads=[dsq, d_ones], writes=[dpt])
        for ti, (t0, tn) in enumerate(tiles):
            pt, dpt = banks[ti]
            k.op("dve", lambda e, pt=pt, t0=t0, tn=tn: e.tensor_scalar(
                rstd[:, t0:t0 + tn], pt[:, 0:tn], inv_n, EPS, ALU.mult, ALU.add),
                reads=[dpt], writes=[d_rstd])
            k.op("act", lambda e, t0=t0, tn=tn: e.activation(
                rstd[:, t0:t0 + tn], rstd[:, t0:t0 + tn], AF.Sqrt), reads=[d_rstd], writes=[d_rstd])
            k.op("dve", lambda e, t0=t0, tn=tn: e.reciprocal(
                rstd[:, t0:t0 + tn], rstd[:, t0:t0 + tn]), reads=[d_rstd], writes=[d_rstd])

    def load_wblk(src, c0):
        t, dt_ = wrot.next()
        k.dma("pool", t[:], src[:, c0:c0 + 256].rearrange("(kc p) n -> p kc n", p=128), writes=[dt_])
        return t, dt_

    for tl in range(NT):
        for kc in range(KC):
            k.dma("sp", x_sb[:, kc, :], xTh[tl, kc * 128:(kc + 1) * 128, :], writes=[d_x[kc]])
        wq = [load_wblk(w_out, 0), load_wblk(w_out, 256)]

        for chunks in (range(0, 4), range(4, 12), range(12, 16)):
            staged = {}

            def src_of(kc, staged=staged):
                t, dt_ = mixrot.next()
                k.dma("sp", t[:], mixTh[tl, kc * 128:(kc + 1) * 128, :], writes=[dt_])
                return (lambda sl, t=t: t[:, sl]), [dt_]

            stats(list(chunks), src_of, 1.0 / (128 * len(chunks)), TT3)
            for kc in chunks:
                t, dt_ = mixrot.next()
                k.dma("sp", t[:], mixTh[tl, kc * 128:(kc + 1) * 128, :], writes=[dt_])
                k.op("dve", lambda e, t=t, kc=kc: e.scalar_tensor_tensor(
                    mh[:, kc, :], t[:], g_sb[:, 0, kc:kc + 1], rstd[:], ALU.mult, ALU.mult),
                    reads=[dt_, d_g, d_rstd], writes=[d_mh])

        for cb in range(D // 256):
            if cb + 2 < D // 256:
                wq.append(load_wblk(w_out, (cb + 2) * 256))
            elif cb + 2 == D // 256:
                wq.append(load_wblk(ffn_up, 0))
            else:
                wq.append(load_wblk(ffn_up, DFF))
            wt, dwt = wq.pop(0)
            for ml in range(2):
                m = cb * 2 + ml
                for (t0, tn) in TT3:
                    pt, dpt = pb.next()
                    for kc in range(KC):
                        k.op("pe", lambda e, pt=pt, wt=wt, ml=ml, kc=kc, t0=t0, tn=tn: e.matmul(
                            pt[:, 0:tn], wt[:, kc, ml * 128:(ml + 1) * 128], mh[:, kc, t0:t0 + tn],
                            start=(kc == 0), stop=(kc == KC - 1)),
                            reads=[dwt, d_mh], writes=[dpt], signal=(kc == KC - 1))
                    k.op("dve", lambda e, pt=pt, m=m, t0=t0, tn=tn: e.tensor_tensor(
                        x_sb[:, m, t0:t0 + tn], x_sb[:, m, t0:t0 + tn], pt[:, 0:tn], ALU.add),
                        reads=[dpt, d_x[m]], writes=[d_x[m]])

        stats(list(range(KC)), lambda kc: ((lambda sl, kc=kc: x_sb[:, kc, sl]), [d_x[kc]]), 1.0 / D, TT3)
        for kc in range(KC):
            k.op("dve", lambda e, kc=kc: e.scalar_tensor_tensor(
                mh[:, kc, :], x_sb[:, kc, :], g_sb[:, 1, kc:kc + 1], rstd[:], ALU.mult, ALU.mult),
                reads=[d_x[kc], d_g, d_rstd], writes=[d_mh])

        NBP = DFF // 256
        for bp in range(NBP):
            if bp + 1 < NBP:
                wq.append(load_wblk(ffn_up, (bp + 1) * 256))
                wq.append(load_wblk(ffn_up, DFF + (bp + 1) * 256))
            wg, dwg = wq.pop(0)
            wu, dwu = wq.pop(0)
            for fl in range(2):
                f = bp * 2 + fl
                fi = f % FG
                grp = f // FG
                if fi == 0:
                    k.dma("pool", dwn[:], ffn_down[grp * FG * 128:(grp + 1) * FG * 128, :].rearrange(
                        "(f p) n -> p f n", p=128), writes=[d_dwn])
                conv = {}
                for kind, (wt, dwt) in (("g", (wg, dwg)), ("u", (wu, dwu))):
                    ft = f if kind == "g" else NF + f
                    banks = [pb.next() for _ in TT3]
                    for ti, (t0, tn) in enumerate(TT3):
                        pt, dpt = banks[ti]
                        for kc in range(KC):
                            k.op("pe", lambda e, pt=pt, wt=wt, fl=fl, kc=kc, t0=t0, tn=tn: e.matmul(
                                pt[:, 0:tn], wt[:, kc, fl * 128:(fl + 1) * 128], mh[:, kc, t0:t0 + tn],
                                start=(kc == 0), stop=(kc == KC - 1)),
                                reads=[dwt, d_mh], writes=[dpt], signal=(kc == KC - 1))
                    h, dh = hrot.next()
                    for ti, (t0, tn) in enumerate(TT3):
                        pt, dpt = banks[ti]
                        k.op("act", lambda e, h=h, pt=pt, t0=t0, tn=tn: e.copy(h[:, t0:t0 + tn], pt[:, 0:tn]),
                             reads=[dpt], writes=[dh])
                    c, dc = (cgrot.next() if kind == "g" else curot.next())
                    k.op("dve", lambda e, c=c, h=h, ft=ft: e.tensor_scalar(
                        c[:], h[:, 0:TOK], cw[:, ft, 0:1], cw[:, ft, 3:4], ALU.mult, ALU.add),
                        reads=[dh, d_cw], writes=[dc])
                    k.op("dve", lambda e, c=c, h=h, ft=ft: e.scalar_tensor_tensor(
                        c[:], h[:, 1:TOK + 1], cw[:, ft, 1:2], c[:], ALU.mult, ALU.add),
                        reads=[dh, d_cw, dc], writes=[dc])
                    k.op("dve", lambda e, c=c, h=h, ft=ft: e.scalar_tensor_tensor(
                        c[:], h[:, 2:TOK + 2], cw[:, ft, 2:3], c[:], ALU.mult, ALU.add),
                        reads=[dh, d_cw, dc], writes=[dc])
                    conv[kind] = (c, dc)
                cg, dcg = conv["g"]
                cu, dcu = conv["u"]
                k.op("act", lambda e, cg=cg: e.activation(cg[:], cg[:], AF.Silu), reads=[dcg], writes=[dcg])
                k.op("dve", lambda e, cg=cg, cu=cu, fi=fi: e.tensor_tensor(
                    act[:, fi, :], cg[:], cu[:], ALU.mult), reads=[dcg, dcu], writes=[d_act])
                if fi == FG - 1:
                    for m in range(KC):
                        for nt in range(2):
                            pt, dpt = py.next()
                            for fj in range(FG):
                                k.op("pe", lambda e, pt=pt, fj=fj, m=m, nt=nt: e.matmul(
                                    pt[:], dwn[:, fj, m * 128:(m + 1) * 128], act[:, fj, nt * 512:(nt + 1) * 512],
                                    start=(fj == 0), stop=(fj == FG - 1)),
                                    reads=[d_dwn, d_act], writes=[dpt], signal=(fj == FG - 1))
                            xs = x_sb[:, m, 1 + nt * 512:1 + (nt + 1) * 512]
                            k.op("dve", lambda e, pt=pt, xs=xs: e.tensor_tensor(xs, xs, pt[:], ALU.add),
                                 reads=[dpt, d_x[m]], writes=[d_x[m]])

        if final:
            stats(list(range(KC)), lambda kc: ((lambda sl, kc=kc: x_sb[:, kc, sl]), [d_x[kc]]), 1.0 / D,
                  ((1, 512), (513, 512)))
            for kc in range(KC):
                k.op("dve", lambda e, kc=kc: e.scalar_tensor_tensor(
                    x_sb[:, kc, 1:TOK + 1], x_sb[:, kc, 1:TOK + 1], g_sb[:, 2, kc:kc + 1], rstd[:, 1:TOK + 1],
                    ALU.mult, ALU.mult), reads=[d_x[kc], d_g, d_rstd], writes=[d_x[kc]])
        for kc in range(KC):
            k.out_dma("sp", o_xT[tl, kc * 128:(kc + 1) * 128, :], x_sb[:, kc, 1:TOK + 1], reads=[d_x[kc]])
    k.emit()
    return nc


def build_F(NT=4):
    nc = bass.Bass("TRN2", target_bir_lowering=False)
    xT = nc.dram_tensor("xT", [NT, D, TOK], F32, kind="ExternalInput").ap()
    gn = nc.dram_tensor("gn", [128, KC], F32, kind="ExternalInput").ap()
    o_xT = nc.dram_tensor("o_xT", [NT, D, TOK], F32, kind="ExternalOutput").ap()
    k = KB(nc)
    x_sb = k.sb("x_sb", [128, KC, TOK], F32)
    g_sb = k.sb("g_sb", [128, KC], F32)
    ones_bf = k.sb("ones_bf", [128, 128], BF16)
    rstd = k.sb("rstd", [128, TOK], F32)
    d_x = k.deps(KC, "x")
    d_g, d_ones, d_rstd = k.dep("g"), k.dep("ones"), k.dep("rstd")
    sqrot = Rot(k, "sq", 3, [128, 512], BF16)
    psrot = Rot(k, "ps", 4, [128, 512], F32, psum=True)
    k.dma("sp", g_sb[:], gn, writes=[d_g])
    k.op("dve", lambda e: e.memset(ones_bf[:], 1.0), writes=[d_ones])
    for tl in range(NT):
        for kc in range(KC):
            k.dma("sp", x_sb[:, kc, :], xT[tl, kc * 128:(kc + 1) * 128, :], writes=[d_x[kc]])
        rms_stats(k, lambda kc, sl: x_sb[:, kc, sl], KC, [(0, 512), (512, 512)], ones_bf, d_ones,
                  sqrot, psrot, rstd, d_rstd, 1.0 / D, d_x)
        for kc in range(KC):
            k.op("dve", lambda e, kc=kc: e.scalar_tensor_tensor(
                x_sb[:, kc, :], x_sb[:, kc, :], g_sb[:, kc:kc + 1], rstd[:], ALU.mult, ALU.mult),
                reads=[d_x[kc], d_g, d_rstd], writes=[d_x[kc]])
            k.out_dma("pool", o_xT[tl, kc * 128:(kc + 1) * 128, :], x_sb[:, kc, :], reads=[d_x[kc]])
    k.emit()
    return nc


_DEBUG = {}
_CONST = {}


def _t5_bucket(rel):
    half = 16
    max_exact = 8
    n = np.abs(rel)
    nl = np.maximum(n, max_exact).astype(np.float32)
    large = max_exact + (np.log(nl / max_exact) / np.log(1024 / max_exact)
                         * (half - max_exact)).astype(np.int32)
    large = np.minimum(large, half - 1)
    b = np.where(n < max_exact, n, large) + (rel > 0).astype(np.int32) * half
    return b.astype(np.int32)


def _constants():
    if _CONST:
        return _CONST
    s = np.arange(SEQ, dtype=np.int64)
    ang = 2.0 * np.pi * ((s[:, None] * s[None, :]) % SEQ).astype(np.float64) / SEQ
    C = np.cos(ang).astype(NPBF)
    S = np.sin(ang).astype(NPBF)
    _CONST["ct"] = [np.ascontiguousarray(C[:, c * 512:(c + 1) * 512].reshape(32, 128, 512).transpose(1, 0, 2))
                    for c in range(NCORE)]
    _CONST["st"] = [np.ascontiguousarray(S[:, c * 512:(c + 1) * 512].reshape(32, 128, 512).transpose(1, 0, 2))
                    for c in range(NCORE)]
    c64 = np.arange(64, dtype=np.int64)
    a64 = 2.0 * np.pi * ((c64[:, None] * c64[None, :]) % 64).astype(np.float64) / 64
    bdc = np.zeros((128, 128), np.float64)
    bds = np.zeros((128, 128), np.float64)
    for g in range(2):
        bdc[g * 64:(g + 1) * 64, g * 64:(g + 1) * 64] = np.cos(a64)
        bds[g * 64:(g + 1) * 64, g * 64:(g + 1) * 64] = -np.sin(a64)
    _CONST["bdc"] = bdc.astype(NPBF)
    _CONST["bds"] = bds.astype(NPBF)
    sel = np.zeros((128, 128), np.float32)
    sel[64, :] = 1.0
    _CONST["sel"] = sel
    perms = []
    for (_, d) in PATTERNS:
        L = SEQ // d
        perms.append((np.arange(d)[:, None] + d * np.arange(L)[None, :]).reshape(-1))
    _CONST["perm"] = perms
    kk = np.arange(128)[:, None, None]
    jj = np.arange(2)[None, :, None]
    qq = np.arange(128)[None, None, :]
    rel = kk + 128 * jj - 64 - qq
    _CONST["rel_valid"] = np.abs(rel) <= 64
    _CONST["rel_bucket"] = [_t5_bucket(np.clip(rel, -64, 64) * d) for (_, d) in PATTERNS]
    return _CONST


def _run(nc, maps):
    res = run_bass_kernel_spmd(nc, maps, core_ids=list(range(len(maps))))
    return res.results


def _bias_tiles(rel_bias, c):
    cst = _constants()
    out = np.empty((128, 12, 512), np.float32)
    for i in range(4):
        h = (c % 4) * 4 + i
        for p in range(3):
            t = rel_bias[cst["rel_bucket"][p], h].astype(np.float32)
            t = np.where(cst["rel_valid"], t, np.float32(-30000.0))
            out[:, i * 3 + p, :] = np.concatenate([t.reshape(128, 256)] * 2, axis=1)
    return out


def _layer(l, xT, P):
    cst = _constants()
    gn = np.ascontiguousarray(P["norm_mix"][l].reshape(KC, 128).T)
    wsT = np.ascontiguousarray(np.transpose(P["gmlp_ws"][l], (2, 0, 1)))
    gb = np.zeros((128, 4, 512), np.float32)
    for pr in range(4):
        for hl in range(2):
            gb[hl * 64:(hl + 1) * 64, pr, :] = np.tile(P["gmlp_b"][l][2 * pr + hl], 4)[None, :]
    w_in_l = np.ascontiguousarray(P["w_in"][l])
    NT = SEQ // TOK
    NTT = NB * NT
    xs = np.stack([xT[b][:, tl * TOK:(tl + 1) * TOK] for b in range(NB) for tl in range(NT)])
    rA = _run(build_A(NTT), [{"xT": np.ascontiguousarray(xs), "w_in": w_in_l, "gn": gn,
                              "wsT": wsT, "gb": gb}])[0]
    qT = [np.concatenate(list(rA["o_qT"][b * NT:(b + 1) * NT]), axis=1) for b in range(NB)]
    kT = [np.concatenate(list(rA["o_kT"][b * NT:(b + 1) * NT]), axis=1) for b in range(NB)]
    va = [np.concatenate(list(rA["o_va"][b * NT:(b + 1) * NT]), axis=0) for b in range(NB)]
    zc = [np.concatenate(list(rA["o_zc"][b * NT:(b + 1) * NT]), axis=0) for b in range(NB)]
    aT = [np.concatenate(list(rA["o_aT"][b * NT:(b + 1) * NT]), axis=1) for b in range(NB)]
    maps = []
    for c in range(NCORE):
        b = c // 4
        m = {"bias": _bias_tiles(P["rel_bias"], c), "sel": cst["sel"]}
        for pp in range(2):
            gpp = (c % 4) * 2 + pp
            rows = slice(gpp * 128, (gpp + 1) * 128)
            for p, (_, d) in enumerate(PATTERNS):
                L, PL = pat_geom(d)
                perm = cst["perm"][p]
                qp = qT[b][rows][:, perm]
                qa = qp.copy()
                qa[64:] = 0
                qb = qp.copy()
                qb[:64] = 0
                kp = np.zeros((128, d, L + 128), NPBF)
                kp[:, :, 64:64 + L] = kT[b][rows][:, perm].reshape(128, d, L)
                vv = np.zeros((2, d, L + 128, 65), NPBF)
                for hl in range(2):
                    h = gpp * 2 + hl
                    vv[hl, :, 64:64 + L, :64] = va[b][perm, h * 64:(h + 1) * 64].reshape(d, L, 64)
                vv[:, :, :, 64] = 1
                vv = vv.reshape(2, PL // 128, 128, 65).transpose(2, 0, 1, 3)
                m[f"k_{pp}_{p}"] = np.ascontiguousarray(kp.reshape(128, PL))
                m[f"qa_{pp}_{p}"] = np.ascontiguousarray(qa)
                m[f"qb_{pp}_{p}"] = np.ascontiguousarray(qb)
                m[f"v_{pp}_{p}"] = np.ascontiguousarray(vv)
        maps.append(m)
    rB1 = _run(build_B1(), maps)
    bT = [np.concatenate([rB1[b * 4 + i]["o_bT"] for i in range(4)], axis=0) for b in range(NB)]
    zcl = np.stack([zc[b].reshape(32, 128, 4, 128).transpose(2, 1, 0, 3) for b in range(NB)])
    zcl = np.ascontiguousarray(zcl.reshape(NB * 4, 128, 32, 128))
    bdw = np.zeros((128, 4, 128), np.float32)
    for gp in range(4):
        for g in range(2):
            bdw[g * 64:(g + 1) * 64, gp, g * 64:(g + 1) * 64] = P["fnet_w"][l][2 * gp + g]
    maps = [{"zc": zcl, "ct": cst["ct"][c], "st": cst["st"][c], "bdc": cst["bdc"], "bds": cst["bds"],
             "bdw": bdw} for c in range(NCORE)]
    rB2 = _run(build_B2(), maps)
    cT = [np.empty((512, SEQ), np.float32) for _ in range(NB)]
    for c in range(NCORE):
        o = rB2[c]["o_cT"]
        for u in range(NB * 4):
            b, gp = u // 4, u % 4
            cT[b][gp * 128:(gp + 1) * 128, c * 512:(c + 1) * 512] = o[u]
    gains = np.empty((128, 3, KC), np.float32)
    gains[:, 0, :] = P["mix_gain"][l].reshape(KC, 128).T
    gains[:, 1, :] = P["norm_ffn"][l].reshape(KC, 128).T
    gains[:, 2, :] = P["final_norm"].reshape(KC, 128).T
    convw = np.empty((128, 2 * NF, 4), np.float32)
    for j in range(3):
        convw[:, :, j] = P["ffn_conv_w"][l][j].reshape(2 * NF, 128).T
    convw[:, :, 3] = P["ffn_conv_b"][l].reshape(2 * NF, 128).T
    w_out_l = np.ascontiguousarray(P["w_out"][l])
    up_l = np.ascontiguousarray(P["ffn_up"][l])
    down_l = np.ascontiguousarray(P["ffn_down"][l])
    mixT = [np.concatenate([aT[b], bT[b], cT[b]], axis=0) for b in range(NB)]
    if _DEBUG is not None and _DEBUG.get("on"):
        _DEBUG[f"mixT{l}"] = mixT
    xh = np.zeros((NTT, D, TH), np.float32)
    mh = np.zeros((NTT, D, TH), np.float32)
    for b in range(NB):
        for tl in range(NT):
            t0 = tl * TOK
            lo, hi = max(t0 - 1, 0), min(t0 + TOK + 1, SEQ)
            xh[b * NT + tl][:, lo - (t0 - 1):hi - (t0 - 1)] = xT[b][:, lo:hi]
            mh[b * NT + tl][:, lo - (t0 - 1):hi - (t0 - 1)] = mixT[b][:, lo:hi]
    rC = _run(build_C(NTT), [{"xTh": xh, "mixTh": mh, "w_out": w_out_l, "ffn_up": up_l,
                              "ffn_down": down_l, "gains": gains, "convw": convw}])[0]
    new = np.stack([np.concatenate(list(rC["o_xT"][b * NT:(b + 1) * NT]), axis=1) for b in range(NB)])
    return new


def _final_norm(xT, final_norm):
    NT = SEQ // TOK
    gn = np.ascontiguousarray(final_norm.reshape(KC, 128).T)
    xs = np.stack([xT[b][:, tl * TOK:(tl + 1) * TOK] for b in range(NB) for tl in range(NT)])
    r = _run(build_F(NB * NT), [{"xT": np.ascontiguousarray(xs), "gn": gn}])[0]
    return np.stack([np.concatenate(list(r["o_xT"][b * NT:(b + 1) * NT]), axis=1) for b in range(NB)])


def kernel(x, w_in, gmlp_ws, gmlp_b, fnet_w, mix_gain, w_out, norm_mix, norm_ffn,
           ffn_up, ffn_conv_w, ffn_conv_b, ffn_down, rel_bias, final_norm):
    P = dict(w_in=w_in, gmlp_ws=gmlp_ws, gmlp_b=gmlp_b, fnet_w=fnet_w, mix_gain=mix_gain, w_out=w_out,
             norm_mix=norm_mix, norm_ffn=norm_ffn, ffn_up=ffn_up, ffn_conv_w=ffn_conv_w,
             ffn_conv_b=ffn_conv_b, ffn_down=ffn_down, rel_bias=rel_bias, final_norm=final_norm)
    P = {k_: np.asarray(v, np.float32) for k_, v in P.items()}
    xT = np.ascontiguousarray(np.transpose(np.asarray(x, np.float32), (0, 2, 1)))
    nl = _DEBUG.get("nlayers", DEPTH) if _DEBUG.get("on") else DEPTH
    for l in range(nl):
        xT = _layer(l, xT, P)
        if _DEBUG.get("on"):
            _DEBUG[f"xT{l}"] = xT
    if nl == DEPTH:
        xT = _final_norm(xT, P["final_norm"])
    return np.ascontiguousarray(np.transpose(xT, (0, 2, 1))).astype(np.float32)
```

```python
from contextlib import ExitStack
import numpy as np
import ml_dtypes
import concourse.bass as bass
import concourse.mybir as mybir
from concourse.bass_utils import run_bass_kernel_spmd

F32 = mybir.dt.float32
BF16 = mybir.dt.bfloat16
AF = mybir.ActivationFunctionType
ALU = mybir.AluOpType
AX = mybir.AxisListType
NPBF = ml_dtypes.bfloat16

D = 2048
SEQ = 4096
NB = 2
DEPTH = 4
NCORE = 8
TOK = 1024
KC = D // 128
INW = 4608
DFF = 5632
NF = DFF // 128
EPS = 1e-6
PATTERNS = ((128, 1), (512, 4), (2048, 16))

SEM_LIMIT = 30000
N_DMA_SEMS = 24


class Dep:
    __slots__ = ("name", "writer", "readers")

    def __init__(self, name=""):
        self.name = name
        self.writer = None
        self.readers = {}


class KB:
    ENGS = ("pe", "act", "dve", "pool", "sp")

    def __init__(self, nc):
        self.nc = nc
        self.es = ExitStack()
        self.q = {e: [] for e in self.ENGS}
        self.cnt = {e: 0 for e in self.ENGS}
        self.epoch = {e: 0 for e in self.ENGS}
        self.sems = {}
        self.waited = {e: {} for e in self.ENGS}
        self.dma_slots = {}
        self.dma_next = {}
        self.ndep = 0
        self.n_inst = 0
        self.out_evs = []

    def sb(self, name, shape, dtype):
        return self.es.enter_context(self.nc.sbuf_tensor(name, list(shape), dtype))

    def ps(self, name, shape, dtype=F32):
        return self.es.enter_context(self.nc.psum_tensor(name, list(shape), dtype))

    def dep(self, name=""):
        self.ndep += 1
        return Dep(name or f"d{self.ndep}")

    def deps(self, n, name=""):
        return [self.dep(f"{name}{i}") for i in range(n)]

    def _sem(self, key):
        if key not in self.sems:
            self.sems[key] = self.es.enter_context(
                self.nc.semaphore(f"s_{key[0]}_{key[1]}"))
        return self.sems[key]

    def _wait(self, eng, ev):
        key, val = ev
        w = self.waited[eng]
        if w.get(key, 0) >= val:
            return
        w[key] = val
        sem = self._sem(key)
        self.q[eng].append(lambda e, sem=sem, val=val: e.wait_ge(sem, val))

    def _collect(self, eng, reads, writes, skip_same_engine=False):
        evs = []
        for d in reads:
            if d.writer is not None:
                evs.append(d.writer)
        for d in writes:
            if d.writer is not None:
                evs.append(d.writer)
            evs.extend(d.readers.values())
        for ev in evs:
            if skip_same_engine and ev[0][0] == eng:
                continue
            self._wait(eng, ev)

    def _record(self, ev, rkey, reads, writes):
        for d in reads:
            d.readers[rkey] = ev
        for d in writes:
            d.writer = ev
            d.readers = {}

    def op(self, eng, fn, reads=(), writes=(), signal=True):
        self._collect(eng, reads, writes, skip_same_engine=(eng == "pe"))
        self.n_inst += 1
        if not signal:
            if self.cnt[eng] >= SEM_LIMIT:
                ev = ((eng, self.epoch[eng] + 1), 1)
            else:
                ev = ((eng, self.epoch[eng]), self.cnt[eng] + 1)
            self.q[eng].append(lambda e, fn=fn: fn(e))
            self._record(ev, eng, reads, writes)
            return None
        if self.cnt[eng] >= SEM_LIMIT:
            self.epoch[eng] += 1
            self.cnt[eng] = 0
        self.cnt[eng] += 1
        key = (eng, self.epoch[eng])
        sem = self._sem(key)
        ev = (key, self.cnt[eng])
        self.q[eng].append(lambda e, fn=fn, sem=sem: fn(e).then_inc(sem, 1))
        self._record(ev, eng, reads, writes)
        return ev

    def dma(self, queue, out_ap, in_ap, reads=(), writes=(), **kw):
        if queue not in self.dma_slots:
            self.dma_slots[queue] = [0] * N_DMA_SEMS
            self.dma_next[queue] = 0
        i = self.dma_next[queue]
        self.dma_next[queue] = (i + 1) % N_DMA_SEMS
        key = ("dma_" + queue, i)
        prev = self.dma_slots[queue][i]
        if prev:
            self._wait(queue, (key, prev))
        self._collect(queue, reads, writes)
        val = prev + 16
        self.dma_slots[queue][i] = val
        sem = self._sem(key)
        self.n_inst += 1
        self.q[queue].append(
            lambda e, o=out_ap, a=in_ap, sem=sem, kw=kw:
            e.dma_start(out=o, in_=a, **kw).then_inc(sem, 16))
        ev = (key, val)
        self._record(ev, key, reads, writes)
        return ev

    def out_dma(self, queue, out_ap, in_ap, reads=()):
        ev = self.dma(queue, out_ap, in_ap, reads=reads)
        self.out_evs.append(ev)
        return ev

    def emit(self):
        for ev in self.out_evs:
            self._wait("sp", ev)
        nc = self.nc
        q = self.q
        with nc.Block() as block:
            @block.sync
            def _(e):
                for f in q["sp"]:
                    f(e)

            @block.tensor
            def _(e):
                for f in q["pe"]:
                    f(e)

            @block.scalar
            def _(e):
                for f in q["act"]:
                    f(e)

            @block.vector
            def _(e):
                for f in q["dve"]:
                    f(e)

            @block.gpsimd
            def _(e):
                for f in q["pool"]:
                    f(e)
        self.es.close()


class Rot:
    def __init__(self, k, name, n, shape, dtype, psum=False):
        self.bufs = []
        for i in range(n):
            t = k.ps(f"{name}{i}", shape, dtype) if psum else k.sb(f"{name}{i}", shape, dtype)
            self.bufs.append((t, k.dep(f"{name}{i}")))
        self.i = 0

    def next(self):
        b = self.bufs[self.i]
        self.i = (self.i + 1) % len(self.bufs)
        return b


def rms_stats(k, src_fn, nchunk, ntok_tiles, ones_bf, d_ones, sq_rot, ps_rot, rstd, d_rstd,
              inv_n, src_deps):
    for (t0, tn) in ntok_tiles:
        pt, dpt = ps_rot.next()
        for kc in range(nchunk):
            sq, dsq = sq_rot.next()
            k.op("act", lambda e, sq=sq, kc=kc, t0=t0, tn=tn: e.activation(
                sq[:, 0:tn], src_fn(kc, slice(t0, t0 + tn)), AF.Square),
                reads=src_deps, writes=[dsq])
            k.op("pe", lambda e, pt=pt, sq=sq, kc=kc, tn=tn: e.matmul(
                pt[:, 0:tn], ones_bf[:], sq[:, 0:tn], start=(kc == 0), stop=(kc == nchunk - 1)),
                reads=[dsq, d_ones], writes=[dpt], signal=True)
        k.op("dve", lambda e, pt=pt, t0=t0, tn=tn: e.tensor_scalar(
            rstd[:, t0:t0 + tn], pt[:, 0:tn], inv_n, EPS, ALU.mult, ALU.add),
            reads=[dpt], writes=[d_rstd])
        k.op("act", lambda e, t0=t0, tn=tn: e.activation(
            rstd[:, t0:t0 + tn], rstd[:, t0:t0 + tn], AF.Sqrt),
            reads=[d_rstd], writes=[d_rstd])
        k.op("dve", lambda e, t0=t0, tn=tn: e.reciprocal(
            rstd[:, t0:t0 + tn], rstd[:, t0:t0 + tn]),
            reads=[d_rstd], writes=[d_rstd])


def build_A(NT=4):
    nc = bass.Bass("TRN2", target_bir_lowering=False)
    xT = nc.dram_tensor("xT", [NT, D, TOK], F32, kind="ExternalInput").ap()
    w_in = nc.dram_tensor("w_in", [D, INW], F32, kind="ExternalInput").ap()
    gn = nc.dram_tensor("gn", [128, KC], F32, kind="ExternalInput").ap()
    wsT = nc.dram_tensor("wsT", [128, 8, 128], F32, kind="ExternalInput").ap()
    gb = nc.dram_tensor("gb", [128, 4, 512], F32, kind="ExternalInput").ap()
    o_qT = nc.dram_tensor("o_qT", [NT] + [1024, TOK], BF16, kind="ExternalOutput").ap()
    o_kT = nc.dram_tensor("o_kT", [NT] + [1024, TOK], BF16, kind="ExternalOutput").ap()
    o_va = nc.dram_tensor("o_va", [NT] + [TOK, 1024], BF16, kind="ExternalOutput").ap()
    o_zc = nc.dram_tensor("o_zc", [NT] + [TOK, 512], BF16, kind="ExternalOutput").ap()
    o_aT = nc.dram_tensor("o_aT", [NT] + [512, TOK], F32, kind="ExternalOutput").ap()

    k = KB(nc)
    x_sb = k.sb("x_sb", [128, KC, TOK], F32)
    xn = k.sb("xn", [128, KC, TOK], BF16)
    g_sb = k.sb("g_sb", [128, KC], F32)
    ws_sb = k.sb("ws_sb", [128, 8, 128], BF16)
    gb_sb = k.sb("gb_sb", [128, 4, 512], F32)
    ones_bf = k.sb("ones_bf", [128, 128], BF16)
    rstd = k.sb("rstd", [128, TOK], F32)
    uT = k.sb("uT", [128, 4, TOK], F32)
    vtok = k.sb("vtok", [128, 8, 512], BF16)
    d_x = k.deps(KC, "x")
    d_xn, d_g, d_ws, d_gb, d_ones, d_rstd = (k.dep(n) for n in
                                              ("xn", "g", "ws", "gb", "ones", "rstd"))
    d_u = k.deps(4, "u")
    d_v = k.deps(8, "v")
    d_a = k.deps(4, "a")
    wrot = Rot(k, "wblk", 3, [128, KC, 512], BF16)
    sqrot = Rot(k, "sq", 3, [128, 512], BF16)
    psrot = Rot(k, "ps", 4, [128, 512], F32, psum=True)
    gps = Rot(k, "gps", 4, [128, 512], F32, psum=True)
    stf = Rot(k, "stf", 3, [128, TOK], BF16)
    stt = Rot(k, "stt", 3, [128, 512], BF16)
    gtmp = Rot(k, "gtmp", 2, [128, 512], F32)

    k.dma("sp", g_sb[:], gn, writes=[d_g])
    k.dma("sp", gb_sb[:], gb, writes=[d_gb])
    k.dma("pool", ws_sb[:], wsT, writes=[d_ws])
    k.op("dve", lambda e: e.memset(ones_bf[:], 1.0), writes=[d_ones])

    for tl in range(NT):
        for kc in range(KC):
            k.dma("sp", x_sb[:, kc, :], xT[tl, kc * 128:(kc + 1) * 128, :], writes=[d_x[kc]])
        NBLK = INW // 512
        wtiles = {}

        def load_w(b):
            t, dt_ = wrot.next()
            k.dma("pool", t[:], w_in[:, b * 512:(b + 1) * 512].rearrange("(kc p) n -> p kc n", p=128),
                  writes=[dt_])
            wtiles[b] = (t, dt_)

        load_w(0)
        load_w(1)

        rms_stats(k, lambda kc, sl: x_sb[:, kc, sl], KC, [(0, 512), (512, 512)], ones_bf, d_ones,
                  sqrot, psrot, rstd, d_rstd, 1.0 / D, d_x)
        for kc in range(KC):
            for h in range(2):
                sl = slice(h * 512, (h + 1) * 512)
                k.op("dve", lambda e, kc=kc, sl=sl: e.scalar_tensor_tensor(
                    xn[:, kc, sl], x_sb[:, kc, sl], g_sb[:, kc:kc + 1], rstd[:, sl],
                    ALU.mult, ALU.mult),
                    reads=[d_x[kc], d_g, d_rstd], writes=[d_xn])

        ei = [0]

        def evac_copy(out_ap, in_ap, reads, writes, scale=None):
            ei[0] += 1
            if ei[0] % 2 == 0:
                if scale is None:
                    k.op("act", lambda e: e.copy(out_ap, in_ap), reads=reads, writes=writes)
                else:
                    k.op("act", lambda e: e.mul(out_ap, in_ap, scale), reads=reads, writes=writes)
            else:
                if scale is None:
                    k.op("dve", lambda e: e.tensor_copy(out_ap, in_ap), reads=reads, writes=writes)
                else:
                    k.op("dve", lambda e: e.tensor_scalar(out_ap, in_ap, scale, None, ALU.mult),
                         reads=reads, writes=writes)

        for b in range(NBLK):
            if b + 2 < NBLK:
                load_w(b + 2)
            wt, dwt = wtiles.pop(b)
            feat_major = b in (0, 2, 3, 4, 5)
            if feat_major:
                for m in range(4):
                    if b != 0:
                        st, dst = stf.next()
                    for nt in range(2):
                        pt, dpt = psrot.next()
                        for kc in range(KC):
                            k.op("pe", lambda e, pt=pt, wt=wt, m=m, kc=kc, nt=nt: e.matmul(
                                pt[:], wt[:, kc, m * 128:(m + 1) * 128],
                                xn[:, kc, nt * 512:(nt + 1) * 512],
                                start=(kc == 0), stop=(kc == KC - 1)),
                                reads=[dwt, d_xn], writes=[dpt], signal=(kc == KC - 1))
                        sl = slice(nt * 512, (nt + 1) * 512)
                        if b == 0:
                            k.op("act", lambda e, pt=pt, m=m, sl=sl: e.activation(
                                uT[:, m, sl], pt[:], AF.Gelu), reads=[dpt], writes=[d_u[m]])
                        else:
                            evac_copy(st[:, sl], pt[:], [dpt], [dst],
                                      scale=(0.125 if b in (2, 3) else None))
                    if b != 0:
                        row0 = ((b - 2) % 2) * 512 + m * 128
                        dst_t = o_qT if b in (2, 3) else o_kT
                        k.out_dma("sp", dst_t[tl, row0:row0 + 128, :], st[:], reads=[dst])
            else:
                for tt in range(8):
                    pt, dpt = psrot.next()
                    for kc in range(KC):
                        k.op("pe", lambda e, pt=pt, wt=wt, tt=tt, kc=kc: e.matmul(
                            pt[:], xn[:, kc, tt * 128:(tt + 1) * 128], wt[:, kc, :],
                            start=(kc == 0), stop=(kc == KC - 1)),
                            reads=[dwt, d_xn], writes=[dpt], signal=(kc == KC - 1))
                    if b == 1:
                        k.op("act", lambda e, pt=pt, tt=tt: e.activation(
                            vtok[:, tt, :], pt[:], AF.Gelu), reads=[dpt], writes=[d_v[tt]])
                    else:
                        st, dst = stt.next()
                        evac_copy(st[:], pt[:], [dpt], [dst])
                        if b in (6, 7):
                            k.out_dma("sp", o_va[tl, tt * 128:(tt + 1) * 128, (b - 6) * 512:(b - 5) * 512],
                                      st[:], reads=[dst])
                        else:
                            k.out_dma("sp", o_zc[tl, tt * 128:(tt + 1) * 128, :], st[:], reads=[dst])
            if b == 1:
                for pr in range(4):
                    for nt in range(2):
                        pa, dpa = gps.next()
                        pb, dpb = gps.next()
                        for c4 in range(4):
                            tt = nt * 4 + c4
                            for (pp, dpp, h) in ((pa, dpa, 2 * pr), (pb, dpb, 2 * pr + 1)):
                                k.op("pe", lambda e, pp=pp, tt=tt, pr=pr, h=h, c4=c4: e.matmul(
                                    pp[:, c4 * 128:(c4 + 1) * 128],
                                    vtok[:, tt, pr * 128:(pr + 1) * 128], ws_sb[:, h, :],
                                    start=True, stop=True),
                                    reads=[d_v[tt], d_ws], writes=[dpp], signal=(c4 == 3))
                        tmp, dtmp = gtmp.next()
                        sl = slice(nt * 512, (nt + 1) * 512)
                        for (pp, dpp, lo) in ((pa, dpa, 0), (pb, dpb, 64)):
                            k.op("dve", lambda e, pp=pp, lo=lo, tmp=tmp, pr=pr: e.tensor_tensor(
                                tmp[lo:lo + 64, :], pp[lo:lo + 64, :], gb_sb[lo:lo + 64, pr, :], ALU.add),
                                reads=[dpp, d_gb], writes=[dtmp])
                        k.op("dve", lambda e, tmp=tmp, pr=pr, sl=sl: e.tensor_tensor(
                            uT[:, pr, sl], tmp[:], uT[:, pr, sl], ALU.mult),
                            reads=[dtmp, d_u[pr]], writes=[d_u[pr]])
                    k.out_dma("sp", o_aT[tl, pr * 128:(pr + 1) * 128, :], uT[:, pr, :], reads=[d_u[pr]])
    k.emit()
    return nc


def pat_geom(d):
    L = SEQ // d
    return L, d * (L + 128)


def build_B1():
    nc = bass.Bass("TRN2", target_bir_lowering=False)
    ins = {}
    for pp in range(2):
        for p, (_, d) in enumerate(PATTERNS):
            L, PL = pat_geom(d)
            ins[("k", pp, p)] = nc.dram_tensor(f"k_{pp}_{p}", [128, PL], BF16, kind="ExternalInput").ap()
            ins[("qa", pp, p)] = nc.dram_tensor(f"qa_{pp}_{p}", [128, SEQ], BF16, kind="ExternalInput").ap()
            ins[("qb", pp, p)] = nc.dram_tensor(f"qb_{pp}_{p}", [128, SEQ], BF16, kind="ExternalInput").ap()
            ins[("v", pp, p)] = nc.dram_tensor(f"v_{pp}_{p}", [128, 2, PL // 128, 65], BF16,
                                               kind="ExternalInput").ap()
    bias = nc.dram_tensor("bias", [128, 12, 512], F32, kind="ExternalInput").ap()
    selin = nc.dram_tensor("sel", [128, 128], F32, kind="ExternalInput").ap()
    o_bT = nc.dram_tensor("o_bT", [256, SEQ], F32, kind="ExternalOutput").ap()

    k = KB(nc)
    PLMAX = pat_geom(16)[1]
    krot = Rot(k, "k_sb", 2, [128, PLMAX], BF16)
    qarot = Rot(k, "qa_sb", 2, [128, SEQ], BF16)
    qbrot = Rot(k, "qb_sb", 2, [128, SEQ], BF16)
    vrot = Rot(k, "v_sb", 2, [128, 2, PLMAX // 128, 65], BF16)
    E = k.sb("E", [128, 12, 512], F32)
    d_E = k.dep("E")
    sel = k.sb("sel_sb", [128, 128], F32)
    d_sel = k.dep("sel")
    acc = [k.sb(f"acc{i}", [128, SEQ], F32) for i in range(2)]
    d_acc = k.deps(2, "acc")
    srot = Rot(k, "S", 3, [128, 512], F32, psum=True)
    orot = Rot(k, "O", 3, [128, 256], F32, psum=True)
    bcrot = Rot(k, "bc", 2, [128, 512], F32, psum=True)
    pfrot = Rot(k, "Pf", 3, [128, 512], F32)
    pbrot = Rot(k, "Pb", 3, [128, 512], BF16)
    recrot = Rot(k, "rec", 2, [64, 512], F32)
    strot = Rot(k, "st", 2, [64, 512], F32)

    k.dma("sp", E[:], bias, writes=[d_E])
    k.dma("sp", sel[:], selin, writes=[d_sel])
    for i in range(12):
        k.op("act", lambda e, i=i: e.activation(E[:, i, :], E[:, i, :], AF.Exp),
             reads=[d_E], writes=[d_E])

    loaded = {}

    def load(pp, p):
        d = PATTERNS[p][1]
        L, PL = pat_geom(d)
        kt, dk = krot.next()
        qa, dqa = qarot.next()
        qb, dqb = qbrot.next()
        vt, dv = vrot.next()
        k.dma("sp", kt[:, 0:PL], ins[("k", pp, p)], writes=[dk])
        k.dma("sp", qa[:], ins[("qa", pp, p)], writes=[dqa])
        k.dma("sp", qb[:], ins[("qb", pp, p)], writes=[dqb])
        k.dma("sp", vt[:, :, 0:PL // 128, :], ins[("v", pp, p)], writes=[dv])
        loaded[(pp, p)] = (kt, dk, qa, dqa, qb, dqb, vt, dv)

    order = [(pp, p) for pp in range(2) for p in range(3)]
    load(*order[0])
    for oi, (pp, p) in enumerate(order):
        if oi + 1 < len(order):
            load(*order[oi + 1])
        d = PATTERNS[p][1]
        L, PL = pat_geom(d)
        kt, dk, qa, dqa, qb, dqb, vt, dv = loaded.pop((pp, p))
        nblk = L // 128
        for hl in range(2):
            q_sb, dq = (qa, dqa) if hl == 0 else (qb, dqb)
            ei = (pp * 2 + hl) * 3 + p
            for r in range(d):
                for bp in range(nblk // 2):
                    S, dS = srot.next()
                    for blk in range(2):
                        m0 = (bp * 2 + blk) * 128
                        qcol = r * L + m0
                        kbase = r * (L + 128) + m0
                        for j in range(2):
                            c0 = (blk * 2 + j) * 128
                            k.op("pe", lambda e, S=S, c0=c0, kt=kt, kb=kbase + 128 * j, q_sb=q_sb, qcol=qcol:
                                 e.matmul(S[:, c0:c0 + 128], kt[:, kb:kb + 128], q_sb[:, qcol:qcol + 128],
                                          start=True, stop=True),
                                 reads=[dk, dq], writes=[dS], signal=(blk == 1 and j == 1))
                    Pf, dPf = pfrot.next()
                    k.op("act", lambda e, Pf=Pf, S=S: e.activation(Pf[:], S[:], AF.Exp),
                         reads=[dS], writes=[dPf])
                    Pb, dPb = pbrot.next()
                    k.op("dve", lambda e, Pb=Pb, Pf=Pf, ei=ei: e.tensor_tensor(
                        Pb[:], Pf[:], E[:, ei, :], ALU.mult), reads=[dPf, d_E], writes=[dPb])
                    if bp == 0:
                        k.op("dve", lambda e, Pb=Pb: e.memset(Pb[0:64, 0:128], 0.0), writes=[dPb])
                    if bp == nblk // 2 - 1:
                        k.op("dve", lambda e, Pb=Pb: e.memset(Pb[64:128, 384:512], 0.0), writes=[dPb])
                    O, dO = orot.next()
                    for blk in range(2):
                        m0 = (bp * 2 + blk) * 128
                        kti = (r * (L + 128) + m0) // 128
                        for j in range(2):
                            c0 = (blk * 2 + j) * 128
                            k.op("pe", lambda e, O=O, blk=blk, vt=vt, hl=hl, ti=kti + j, Pb=Pb, c0=c0, j=j:
                                 e.matmul(O[0:65, blk * 128:(blk + 1) * 128], vt[:, hl, ti, :],
                                          Pb[:, c0:c0 + 128], start=(j == 0), stop=(j == 1)),
                                 reads=[dv, dPb], writes=[dO], signal=(blk == 1 and j == 1))
                    s0 = r + d * bp * 256
                    av = acc[hl][0:65, s0:s0 + 255 * d + 1:d] if d > 1 else acc[hl][0:65, s0:s0 + 256]
                    if p == 0:
                        k.op("dve", lambda e, av=av, O=O: e.tensor_copy(av, O[0:65, :]),
                             reads=[dO], writes=[d_acc[hl]])
                    else:
                        k.op("dve", lambda e, av=av, O=O: e.tensor_tensor(av, av, O[0:65, :], ALU.add),
                             reads=[dO, d_acc[hl]], writes=[d_acc[hl]])
        if p == 2:
            for hl in range(2):
                hrow = (pp * 2 + hl) * 64
                for nt in range(SEQ // 512):
                    sl = slice(nt * 512, (nt + 1) * 512)
                    bc, dbc = bcrot.next()
                    k.op("pe", lambda e, bc=bc, hl=hl, sl=sl: e.matmul(
                        bc[:], sel[0:65, :], acc[hl][0:65, sl], start=True, stop=True),
                        reads=[d_sel, d_acc[hl]], writes=[dbc])
                    rec, drec = recrot.next()
                    k.op("dve", lambda e, rec=rec, bc=bc: e.reciprocal(rec[:], bc[0:64, :]),
                         reads=[dbc], writes=[drec])
                    st, dst = strot.next()
                    k.op("dve", lambda e, st=st, rec=rec, hl=hl, sl=sl: e.tensor_tensor(
                        st[:], acc[hl][0:64, sl], rec[:], ALU.mult),
                        reads=[drec, d_acc[hl]], writes=[dst])
                    k.out_dma("pool", o_bT[hrow:hrow + 64, sl], st[:], reads=[dst])
    k.emit()
    return nc


def build_B2():
    nc = bass.Bass("TRN2", target_bir_lowering=False)
    zc = nc.dram_tensor("zc", [NB * 4, 128, 32, 128], BF16, kind="ExternalInput").ap()
    ct = nc.dram_tensor("ct", [128, 32, 512], BF16, kind="ExternalInput").ap()
    stb = nc.dram_tensor("st", [128, 32, 512], BF16, kind="ExternalInput").ap()
    bdc = nc.dram_tensor("bdc", [128, 128], BF16, kind="ExternalInput").ap()
    bds = nc.dram_tensor("bds", [128, 128], BF16, kind="ExternalInput").ap()
    bdw = nc.dram_tensor("bdw", [128, 4, 128], F32, kind="ExternalInput").ap()
    o_cT = nc.dram_tensor("o_cT", [NB * 4, 128, 512], F32, kind="ExternalOutput").ap()

    k = KB(nc)
    Ct = k.sb("Ct", [128, 32, 512], BF16)
    St = k.sb("St", [128, 32, 512], BF16)
    BDC = k.sb("BDC", [128, 128], BF16)
    BDS = k.sb("BDS", [128, 128], BF16)
    BDW = k.sb("BDW", [128, 4, 128], BF16)
    d_ct, d_st, d_bd = k.dep("ct"), k.dep("st"), k.dep("bd")
    zrot = Rot(k, "z", 2, [128, 32, 128], BF16)
    prot = Rot(k, "pp", 6, [128, 512], F32, psum=True)
    pcrot = Rot(k, "pc", 2, [128, 512], BF16)
    psrot = Rot(k, "psb", 2, [128, 512], BF16)
    frot = Rot(k, "f", 2, [128, 512], BF16)
    orot = Rot(k, "o", 2, [128, 512], F32)

    k.dma("sp", BDC[:], bdc, writes=[d_bd])
    k.dma("sp", BDS[:], bds, writes=[d_bd])
    k.dma("pool", BDW[:], bdw, writes=[d_bd])
    for t4 in range(4):
        k.dma("sp", Ct[:, t4 * 8:(t4 + 1) * 8, :], ct[:, t4 * 8:(t4 + 1) * 8, :], writes=[d_ct])
    for t4 in range(4):
        k.dma("sp", St[:, t4 * 8:(t4 + 1) * 8, :], stb[:, t4 * 8:(t4 + 1) * 8, :], writes=[d_st])
    ztiles = {}

    def loadz(u):
        zt, dz = zrot.next()
        k.dma("sp", zt[:], zc[u], writes=[dz])
        ztiles[u] = (zt, dz)

    loadz(0)
    for u in range(NB * 4):
        if u + 1 < NB * 4:
            loadz(u + 1)
        zt, dz = ztiles.pop(u)
        gp = u % 4
        pc, dpc = prot.next()
        for t in range(32):
            k.op("pe", lambda e, pc=pc, zt=zt, t=t: e.matmul(pc[:], zt[:, t, :], Ct[:, t, :],
                                                           start=(t == 0), stop=(t == 31)),
                 reads=[dz, d_ct], writes=[dpc], signal=(t == 31))
        ps_, dps = prot.next()
        for t in range(32):
            k.op("pe", lambda e, ps_=ps_, zt=zt, t=t: e.matmul(ps_[:], zt[:, t, :], St[:, t, :],
                                                             start=(t == 0), stop=(t == 31)),
                 reads=[dz, d_st], writes=[dps], signal=(t == 31))
        pcb, dpcb = pcrot.next()
        k.op("act", lambda e, pcb=pcb, pc=pc: e.copy(pcb[:], pc[:]), reads=[dpc], writes=[dpcb])
        psb, dpsb = psrot.next()
        k.op("dve", lambda e, psb=psb, ps_=ps_: e.tensor_copy(psb[:], ps_[:]), reads=[dps], writes=[dpsb])
        pf, dpf = prot.next()
        k.op("pe", lambda e, pf=pf, pcb=pcb: e.matmul(pf[:], BDC[:], pcb[:], start=True, stop=False),
             reads=[d_bd, dpcb], writes=[dpf], signal=False)
        k.op("pe", lambda e, pf=pf, psb=psb: e.matmul(pf[:], BDS[:], psb[:], start=False, stop=True),
             reads=[d_bd, dpsb], writes=[dpf])
        fb, dfb = frot.next()
        k.op("act", lambda e, fb=fb, pf=pf: e.mul(fb[:], pf[:], 1.0 / 512.0), reads=[dpf], writes=[dfb])
        po, dpo = prot.next()
        k.op("pe", lambda e, po=po, fb=fb, gp=gp: e.matmul(po[:], BDW[:, gp, :], fb[:], start=True, stop=True),
             reads=[d_bd, dfb], writes=[dpo])
        ot, dot_ = orot.next()
        k.op("dve", lambda e, ot=ot, po=po: e.tensor_copy(ot[:], po[:]), reads=[dpo], writes=[dot_])
        k.out_dma("pool", o_cT[u], ot[:], reads=[dot_])
    k.emit()
    return nc


TH = TOK + 2
TT3 = ((0, 342), (342, 342), (684, 342))
FG = 4
NG = NF // FG


def build_C(NT=4):
    final = False
    nc = bass.Bass("TRN2", target_bir_lowering=False)
    xTh = nc.dram_tensor("xTh", [NT, D, TH], F32, kind="ExternalInput").ap()
    mixTh = nc.dram_tensor("mixTh", [NT, D, TH], F32, kind="ExternalInput").ap()
    w_out = nc.dram_tensor("w_out", [D, D], F32, kind="ExternalInput").ap()
    ffn_up = nc.dram_tensor("ffn_up", [D, 2 * DFF], F32, kind="ExternalInput").ap()
    ffn_down = nc.dram_tensor("ffn_down", [DFF, D], F32, kind="ExternalInput").ap()
    gains = nc.dram_tensor("gains", [128, 3, KC], F32, kind="ExternalInput").ap()
    convw = nc.dram_tensor("convw", [128, 2 * NF, 4], F32, kind="ExternalInput").ap()
    o_xT = nc.dram_tensor("o_xT", [NT, D, TOK], F32, kind="ExternalOutput").ap()

    k = KB(nc)
    x_sb = k.sb("x_sb", [128, KC, TH], F32)
    mh = k.sb("mh", [128, KC, TH], BF16)
    g_sb = k.sb("g_sb", [128, 3, KC], F32)
    cw = k.sb("cw", [128, 2 * NF, 4], F32)
    ones_bf = k.sb("ones_bf", [128, 128], BF16)
    rstd = k.sb("rstd", [128, TH], F32)
    act = k.sb("act", [128, FG, TOK], BF16)
    dwn = k.sb("dwn", [128, FG, D], BF16)
    d_x = k.deps(KC, "x")
    d_mh, d_g, d_cw, d_ones, d_rstd, d_act, d_dwn = (
        k.dep(n) for n in ("mh", "g", "cw", "ones", "rstd", "act", "dwn"))
    mixrot = Rot(k, "mixs", 2, [128, TH], F32)
    sqrot = Rot(k, "sq", 3, [128, 512], BF16)
    wrot = Rot(k, "wblk", 4, [128, KC, 256], BF16)
    hrot = Rot(k, "h", 3, [128, TH], F32)
    cgrot = Rot(k, "cg", 2, [128, TOK], F32)
    curot = Rot(k, "cu", 1, [128, TOK], F32)
    pb = Rot(k, "pb", 6, [128, 512], F32, psum=True)
    py = Rot(k, "py", 2, [128, 512], F32, psum=True)

    k.dma("sp", g_sb[:], gains, writes=[d_g])
    k.dma("sp", cw[:], convw, writes=[d_cw])
    k.op("dve", lambda e: e.memset(ones_bf[:], 1.0), writes=[d_ones])

    def stats(chunks, src_of, inv_n, tiles):
        banks = [pb.next() for _ in tiles]
        for ci, kc in enumerate(chunks):
            fn, sdeps = src_of(kc)
            for ti, (t0, tn) in enumerate(tiles):
                sq, dsq = sqrot.next()
                k.op("act", lambda e, sq=sq, fn=fn, t0=t0, tn=tn: e.activation(
                    sq[:, 0:tn], fn(slice(t0, t0 + tn)), AF.Square), reads=sdeps, writes=[dsq])
                pt, dpt = banks[ti]
                k.op("pe", lambda e, pt=pt, sq=sq, tn=tn, ci=ci: e.matmul(
                    pt[:, 0:tn], ones_bf[:], sq[:, 0:tn], start=(ci == 0), stop=(ci == len(chunks) - 1)),
                    reads=[dsq, d_ones], writes=[dpt])
        for ti, (t0, tn) in enumerate(tiles):
            pt, dpt = banks[ti]
            k.op("dve", lambda e, pt=pt, t0=t0, tn=tn: e.tensor_scalar(
                rstd[:, t0:t0 + tn], pt[:, 0:tn], inv_n, EPS, ALU.mult, ALU.add),
                reads=[dpt], writes=[d_rstd])
            k.op("act", lambda e, t0=t0, tn=tn: e.activation(
                rstd[:, t0:t0 + tn], rstd[:, t0:t0 + tn], AF.Sqrt), reads=[d_rstd], writes=[d_rstd])
            k.op("dve", lambda e, t0=t0, tn=tn: e.reciprocal(
                rstd[:, t0:t0 + tn], rstd[:, t0:t0 + tn]), reads=[d_rstd], writes=[d_rstd])

    def load_wblk(src, c0):
        t, dt_ = wrot.next()
        k.dma("pool", t[:], src[:, c0:c0 + 256].rearrange("(kc p) n -> p kc n", p=128), writes=[dt_])
        return t, dt_

    for tl in range(NT):
        for kc in range(KC):
            k.dma("sp", x_sb[:, kc, :], xTh[tl, kc * 128:(kc + 1) * 128, :], writes=[d_x[kc]])
        wq = [load_wblk(w_out, 0), load_wblk(w_out, 256)]

        for chunks in (range(0, 4), range(4, 12), range(12, 16)):
            staged = {}

            def src_of(kc, staged=staged):
                t, dt_ = mixrot.next()
                k.dma("sp", t[:], mixTh[tl, kc * 128:(kc + 1) * 128, :], writes=[dt_])
                return (lambda sl, t=t: t[:, sl]), [dt_]

            stats(list(chunks), src_of, 1.0 / (128 * len(chunks)), TT3)
            for kc in chunks:
                t, dt_ = mixrot.next()
                k.dma("sp", t[:], mixTh[tl, kc * 128:(kc + 1) * 128, :], writes=[dt_])
                k.op("dve", lambda e, t=t, kc=kc: e.scalar_tensor_tensor(
                    mh[:, kc, :], t[:], g_sb[:, 0, kc:kc + 1], rstd[:], ALU.mult, ALU.mult),
                    reads=[dt_, d_g, d_rstd], writes=[d_mh])

        for cb in range(D // 256):
            if cb + 2 < D // 256:
                wq.append(load_wblk(w_out, (cb + 2) * 256))
            elif cb + 2 == D // 256:
                wq.append(load_wblk(ffn_up, 0))
            else:
                wq.append(load_wblk(ffn_up, DFF))
            wt, dwt = wq.pop(0)
            for ml in range(2):
                m = cb * 2 + ml
                for (t0, tn) in TT3:
                    pt, dpt = pb.next()
                    for kc in range(KC):
                        k.op("pe", lambda e, pt=pt, wt=wt, ml=ml, kc=kc, t0=t0, tn=tn: e.matmul(
                            pt[:, 0:tn], wt[:, kc, ml * 128:(ml + 1) * 128], mh[:, kc, t0:t0 + tn],
                            start=(kc == 0), stop=(kc == KC - 1)),
                            reads=[dwt, d_mh], writes=[dpt], signal=(kc == KC - 1))
                    k.op("dve", lambda e, pt=pt, m=m, t0=t0, tn=tn: e.tensor_tensor(
                        x_sb[:, m, t0:t0 + tn], x_sb[:, m, t0:t0 + tn], pt[:, 0:tn], ALU.add),
                        reads=[dpt, d_x[m]], writes=[d_x[m]])

        stats(list(range(KC)), lambda kc: ((lambda sl, kc=kc: x_sb[:, kc, sl]), [d_x[kc]]), 1.0 / D, TT3)
        for kc in range(KC):
            k.op("dve", lambda e, kc=kc: e.scalar_tensor_tensor(
                mh[:, kc, :], x_sb[:, kc, :], g_sb[:, 1, kc:kc + 1], rstd[:], ALU.mult, ALU.mult),
                reads=[d_x[kc], d_g, d_rstd], writes=[d_mh])

        NBP = DFF // 256
        for bp in range(NBP):
            if bp + 1 < NBP:
                wq.append(load_wblk(ffn_up, (bp + 1) * 256))
                wq.append(load_wblk(ffn_up, DFF + (bp + 1) * 256))
            wg, dwg = wq.pop(0)
            wu, dwu = wq.pop(0)
            for fl in range(2):
                f = bp * 2 + fl
                fi = f % FG
                grp = f // FG
                if fi == 0:
                    k.dma("pool", dwn[:], ffn_down[grp * FG * 128:(grp + 1) * FG * 128, :].rearrange(
                        "(f p) n -> p f n", p=128), writes=[d_dwn])
                conv = {}
                for kind, (wt, dwt) in (("g", (wg, dwg)), ("u", (wu, dwu))):
                    ft = f if kind == "g" else NF + f
                    banks = [pb.next() for _ in TT3]
                    for ti, (t0, tn) in enumerate(TT3):
                        pt, dpt = banks[ti]
                        for kc in range(KC):
                            k.op("pe", lambda e, pt=pt, wt=wt, fl=fl, kc=kc, t0=t0, tn=tn: e.matmul(
                                pt[:, 0:tn], wt[:, kc, fl * 128:(fl + 1) * 128], mh[:, kc, t0:t0 + tn],
                                start=(kc == 0), stop=(kc == KC - 1)),
                                reads=[dwt, d_mh], writes=[dpt], signal=(kc == KC - 1))
                    h, dh = hrot.next()
                    for ti, (t0, tn) in enumerate(TT3):
                        pt, dpt = banks[ti]
                        k.op("act", lambda e, h=h, pt=pt, t0=t0, tn=tn: e.copy(h[:, t0:t0 + tn], pt[:, 0:tn]),
                             reads=[dpt], writes=[dh])
                    c, dc = (cgrot.next() if kind == "g" else curot.next())
                    k.op("dve", lambda e, c=c, h=h, ft=ft: e.tensor_scalar(
                        c[:], h[:, 0:TOK], cw[:, ft, 0:1], cw[:, ft, 3:4], ALU.mult, ALU.add),
                        reads=[dh, d_cw], writes=[dc])
                    k.op("dve", lambda e, c=c, h=h, ft=ft: e.scalar_tensor_tensor(
                        c[:], h[:, 1:TOK + 1], cw[:, ft, 1:2], c[:], ALU.mult, ALU.add),
                        reads=[dh, d_cw, dc], writes=[dc])
                    k.op("dve", lambda e, c=c, h=h, ft=ft: e.scalar_tensor_tensor(
                        c[:], h[:, 2:TOK + 2], cw[:, ft, 2:3], c[:], ALU.mult, ALU.add),
                        reads=[dh, d_cw, dc], writes=[dc])
                    conv[kind] = (c, dc)
                cg, dcg = conv["g"]
                cu, dcu = conv["u"]
                k.op("act", lambda e, cg=cg: e.activation(cg[:], cg[:], AF.Silu), reads=[dcg], writes=[dcg])
                k.op("dve", lambda e, cg=cg, cu=cu, fi=fi: e.tensor_tensor(
                    act[:, fi, :], cg[:], cu[:], ALU.mult), reads=[dcg, dcu], writes=[d_act])
                if fi == FG - 1:
                    for m in range(KC):
                        for nt in range(2):
                            pt, dpt = py.next()
                            for fj in range(FG):
                                k.op("pe", lambda e, pt=pt, fj=fj, m=m, nt=nt: e.matmul(
                                    pt[:], dwn[:, fj, m * 128:(m + 1) * 128], act[:, fj, nt * 512:(nt + 1) * 512],
                                    start=(fj == 0), stop=(fj == FG - 1)),
                                    reads=[d_dwn, d_act], writes=[dpt], signal=(fj == FG - 1))
                            xs = x_sb[:, m, 1 + nt * 512:1 + (nt + 1) * 512]
                            k.op("dve", lambda e, pt=pt, xs=xs: e.tensor_tensor(xs, xs, pt[:], ALU.add),
                                 reads=[dpt, d_x[m]], writes=[d_x[m]])

        if final:
            stats(list(range(KC)), lambda kc: ((lambda sl, kc=kc: x_sb[:, kc, sl]), [d_x[kc]]), 1.0 / D,
                  ((1, 512), (513, 512)))
            for kc in range(KC):
                k.op("dve", lambda e, kc=kc: e.scalar_tensor_tensor(
                    x_sb[:, kc, 1:TOK + 1], x_sb[:, kc, 1:TOK + 1], g_sb[:, 2, kc:kc + 1], rstd[:, 1:TOK + 1],
                    ALU.mult, ALU.mult), reads=[d_x[kc], d_g, d_rstd], writes=[d_x[kc]])
        for kc in range(KC):
            k.out_dma("sp", o_xT[tl, kc * 128:(kc + 1) * 128, :], x_sb[:, kc, 1:TOK + 1], reads=[d_x[kc]])
    k.emit()
    return nc


def build_F(NT=4):
    nc = bass.Bass("TRN2", target_bir_lowering=False)
    xT = nc.dram_tensor("xT", [NT, D, TOK], F32, kind="ExternalInput").ap()
    gn = nc.dram_tensor("gn", [128, KC], F32, kind="ExternalInput").ap()
    o_xT = nc.dram_tensor("o_xT", [NT, D, TOK], F32, kind="ExternalOutput").ap()
    k = KB(nc)
    x_sb = k.sb("x_sb", [128, KC, TOK], F32)
    g_sb = k.sb("g_sb", [128, KC], F32)
    ones_bf = k.sb("ones_bf", [128, 128], BF16)
    rstd = k.sb("rstd", [128, TOK], F32)
    d_x = k.deps(KC, "x")
    d_g, d_ones, d_rstd = k.dep("g"), k.dep("ones"), k.dep("rstd")
    sqrot = Rot(k, "sq", 3, [128, 512], BF16)
    psrot = Rot(k, "ps", 4, [128, 512], F32, psum=True)
    k.dma("sp", g_sb[:], gn, writes=[d_g])
    k.op("dve", lambda e: e.memset(ones_bf[:], 1.0), writes=[d_ones])
    for tl in range(NT):
        for kc in range(KC):
            k.dma("sp", x_sb[:, kc, :], xT[tl, kc * 128:(kc + 1) * 128, :], writes=[d_x[kc]])
        rms_stats(k, lambda kc, sl: x_sb[:, kc, sl], KC, [(0, 512), (512, 512)], ones_bf, d_ones,
                  sqrot, psrot, rstd, d_rstd, 1.0 / D, d_x)
        for kc in range(KC):
            k.op("dve", lambda e, kc=kc: e.scalar_tensor_tensor(
                x_sb[:, kc, :], x_sb[:, kc, :], g_sb[:, kc:kc + 1], rstd[:], ALU.mult, ALU.mult),
                reads=[d_x[kc], d_g, d_rstd], writes=[d_x[kc]])
            k.out_dma("pool", o_xT[tl, kc * 128:(kc + 1) * 128, :], x_sb[:, kc, :], reads=[d_x[kc]])
    k.emit()
    return nc


_DEBUG = {}
_CONST = {}


def _t5_bucket(rel):
    half = 16
    max_exact = 8
    n = np.abs(rel)
    nl = np.maximum(n, max_exact).astype(np.float32)
    large = max_exact + (np.log(nl / max_exact) / np.log(1024 / max_exact)
                         * (half - max_exact)).astype(np.int32)
    large = np.minimum(large, half - 1)
    b = np.where(n < max_exact, n, large) + (rel > 0).astype(np.int32) * half
    return b.astype(np.int32)


def _constants():
    if _CONST:
        return _CONST
    s = np.arange(SEQ, dtype=np.int64)
    ang = 2.0 * np.pi * ((s[:, None] * s[None, :]) % SEQ).astype(np.float64) / SEQ
    C = np.cos(ang).astype(NPBF)
    S = np.sin(ang).astype(NPBF)
    _CONST["ct"] = [np.ascontiguousarray(C[:, c * 512:(c + 1) * 512].reshape(32, 128, 512).transpose(1, 0, 2))
                    for c in range(NCORE)]
    _CONST["st"] = [np.ascontiguousarray(S[:, c * 512:(c + 1) * 512].reshape(32, 128, 512).transpose(1, 0, 2))
                    for c in range(NCORE)]
    c64 = np.arange(64, dtype=np.int64)
    a64 = 2.0 * np.pi * ((c64[:, None] * c64[None, :]) % 64).astype(np.float64) / 64
    bdc = np.zeros((128, 128), np.float64)
    bds = np.zeros((128, 128), np.float64)
    for g in range(2):
        bdc[g * 64:(g + 1) * 64, g * 64:(g + 1) * 64] = np.cos(a64)
        bds[g * 64:(g + 1) * 64, g * 64:(g + 1) * 64] = -np.sin(a64)
    _CONST["bdc"] = bdc.astype(NPBF)
    _CONST["bds"] = bds.astype(NPBF)
    sel = np.zeros((128, 128), np.float32)
    sel[64, :] = 1.0
    _CONST["sel"] = sel
    perms = []
    for (_, d) in PATTERNS:
        L = SEQ // d
        perms.append((np.arange(d)[:, None] + d * np.arange(L)[None, :]).reshape(-1))
    _CONST["perm"] = perms
    kk = np.arange(128)[:, None, None]
    jj = np.arange(2)[None, :, None]
    qq = np.arange(128)[None, None, :]
    rel = kk + 128 * jj - 64 - qq
    _CONST["rel_valid"] = np.abs(rel) <= 64
    _CONST["rel_bucket"] = [_t5_bucket(np.clip(rel, -64, 64) * d) for (_, d) in PATTERNS]
    return _CONST


def _run(nc, maps):
    res = run_bass_kernel_spmd(nc, maps, core_ids=list(range(len(maps))))
    return res.results


def _bias_tiles(rel_bias, c):
    cst = _constants()
    out = np.empty((128, 12, 512), np.float32)
    for i in range(4):
        h = (c % 4) * 4 + i
        for p in range(3):
            t = rel_bias[cst["rel_bucket"][p], h].astype(np.float32)
            t = np.where(cst["rel_valid"], t, np.float32(-30000.0))
            out[:, i * 3 + p, :] = np.concatenate([t.reshape(128, 256)] * 2, axis=1)
    return out


def _layer(l, xT, P):
    cst = _constants()
    gn = np.ascontiguousarray(P["norm_mix"][l].reshape(KC, 128).T)
    wsT = np.ascontiguousarray(np.transpose(P["gmlp_ws"][l], (2, 0, 1)))
    gb = np.zeros((128, 4, 512), np.float32)
    for pr in range(4):
        for hl in range(2):
            gb[hl * 64:(hl + 1) * 64, pr, :] = np.tile(P["gmlp_b"][l][2 * pr + hl], 4)[None, :]
    w_in_l = np.ascontiguousarray(P["w_in"][l])
    NT = SEQ // TOK
    maps = []
    for b in range(NB):
        xs = np.stack([xT[b][:, tl * TOK:(tl + 1) * TOK] for tl in range(NT)])
        maps.append({"xT": np.ascontiguousarray(xs), "w_in": w_in_l, "gn": gn, "wsT": wsT, "gb": gb})
    rA = _run(build_A(NT), maps)
    qT = [np.concatenate(list(rA[b]["o_qT"]), axis=1) for b in range(NB)]
    kT = [np.concatenate(list(rA[b]["o_kT"]), axis=1) for b in range(NB)]
    va = [np.concatenate(list(rA[b]["o_va"]), axis=0) for b in range(NB)]
    zc = [np.concatenate(list(rA[b]["o_zc"]), axis=0) for b in range(NB)]
    aT = [np.concatenate(list(rA[b]["o_aT"]), axis=1) for b in range(NB)]
    maps = []
    for c in range(NCORE):
        b = c // 4
        m = {"bias": _bias_tiles(P["rel_bias"], c), "sel": cst["sel"]}
        for pp in range(2):
            gpp = (c % 4) * 2 + pp
            rows = slice(gpp * 128, (gpp + 1) * 128)
            for p, (_, d) in enumerate(PATTERNS):
                L, PL = pat_geom(d)
                perm = cst["perm"][p]
                qp = qT[b][rows][:, perm]
                qa = qp.copy()
                qa[64:] = 0
                qb = qp.copy()
                qb[:64] = 0
                kp = np.zeros((128, d, L + 128), NPBF)
                kp[:, :, 64:64 + L] = kT[b][rows][:, perm].reshape(128, d, L)
                vv = np.zeros((2, d, L + 128, 65), NPBF)
                for hl in range(2):
                    h = gpp * 2 + hl
                    vv[hl, :, 64:64 + L, :64] = va[b][perm, h * 64:(h + 1) * 64].reshape(d, L, 64)
                vv[:, :, :, 64] = 1
                vv = vv.reshape(2, PL // 128, 128, 65).transpose(2, 0, 1, 3)
                m[f"k_{pp}_{p}"] = np.ascontiguousarray(kp.reshape(128, PL))
                m[f"qa_{pp}_{p}"] = np.ascontiguousarray(qa)
                m[f"qb_{pp}_{p}"] = np.ascontiguousarray(qb)
                m[f"v_{pp}_{p}"] = np.ascontiguousarray(vv)
        maps.append(m)
    rB1 = _run(build_B1(), maps)
    bT = [np.concatenate([rB1[b * 4 + i]["o_bT"] for i in range(4)], axis=0) for b in range(NB)]
    zcl = np.stack([zc[b].reshape(32, 128, 4, 128).transpose(2, 1, 0, 3) for b in range(NB)])
    zcl = np.ascontiguousarray(zcl.reshape(NB * 4, 128, 32, 128))
    bdw = np.zeros((128, 4, 128), np.float32)
    for gp in range(4):
        for g in range(2):
            bdw[g * 64:(g + 1) * 64, gp, g * 64:(g + 1) * 64] = P["fnet_w"][l][2 * gp + g]
    maps = [{"zc": zcl, "ct": cst["ct"][c], "st": cst["st"][c], "bdc": cst["bdc"], "bds": cst["bds"],
             "bdw": bdw} for c in range(NCORE)]
    rB2 = _run(build_B2(), maps)
    cT = [np.empty((512, SEQ), np.float32) for _ in range(NB)]
    for c in range(NCORE):
        o = rB2[c]["o_cT"]
        for u in range(NB * 4):
            b, gp = u // 4, u % 4
            cT[b][gp * 128:(gp + 1) * 128, c * 512:(c + 1) * 512] = o[u]
    gains = np.empty((128, 3, KC), np.float32)
    gains[:, 0, :] = P["mix_gain"][l].reshape(KC, 128).T
    gains[:, 1, :] = P["norm_ffn"][l].reshape(KC, 128).T
    gains[:, 2, :] = P["final_norm"].reshape(KC, 128).T
    convw = np.empty((128, 2 * NF, 4), np.float32)
    for j in range(3):
        convw[:, :, j] = P["ffn_conv_w"][l][j].reshape(2 * NF, 128).T
    convw[:, :, 3] = P["ffn_conv_b"][l].reshape(2 * NF, 128).T
    w_out_l = np.ascontiguousarray(P["w_out"][l])
    up_l = np.ascontiguousarray(P["ffn_up"][l])
    down_l = np.ascontiguousarray(P["ffn_down"][l])
    mixT = [np.concatenate([aT[b], bT[b], cT[b]], axis=0) for b in range(NB)]
    if _DEBUG is not None and _DEBUG.get("on"):
        _DEBUG[f"mixT{l}"] = mixT
    maps = []
    for b in range(NB):
        xh = np.zeros((NT, D, TH), np.float32)
        mh = np.zeros((NT, D, TH), np.float32)
        for tl in range(NT):
            t0 = tl * TOK
            lo, hi = max(t0 - 1, 0), min(t0 + TOK + 1, SEQ)
            xh[tl][:, lo - (t0 - 1):hi - (t0 - 1)] = xT[b][:, lo:hi]
            mh[tl][:, lo - (t0 - 1):hi - (t0 - 1)] = mixT[b][:, lo:hi]
        maps.append({"xTh": xh, "mixTh": mh, "w_out": w_out_l, "ffn_up": up_l, "ffn_down": down_l,
                     "gains": gains, "convw": convw})
    rC = _run(build_C(NT), maps)
    new = np.stack([np.concatenate(list(rC[b]["o_xT"]), axis=1) for b in range(NB)])
    return new


def _final_norm(xT, final_norm):
    NT = SEQ // TOK
    gn = np.ascontiguousarray(final_norm.reshape(KC, 128).T)
    maps = [{"xT": np.ascontiguousarray(np.stack([xT[b][:, tl * TOK:(tl + 1) * TOK] for tl in range(NT)])),
             "gn": gn} for b in range(NB)]
    r = _run(build_F(NT), maps)
    return np.stack([np.concatenate(list(r[b]["o_xT"]), axis=1) for b in range(NB)])


def kernel(x, w_in, gmlp_ws, gmlp_b, fnet_w, mix_gain, w_out, norm_mix, norm_ffn,
           ffn_up, ffn_conv_w, ffn_conv_b, ffn_down, rel_bias, final_norm):
    P = dict(w_in=w_in, gmlp_ws=gmlp_ws, gmlp_b=gmlp_b, fnet_w=fnet_w, mix_gain=mix_gain, w_out=w_out,
             norm_mix=norm_mix, norm_ffn=norm_ffn, ffn_up=ffn_up, ffn_conv_w=ffn_conv_w,
             ffn_conv_b=ffn_conv_b, ffn_down=ffn_down, rel_bias=rel_bias, final_norm=final_norm)
    P = {k_: np.asarray(v, np.float32) for k_, v in P.items()}
    xT = np.ascontiguousarray(np.transpose(np.asarray(x, np.float32), (0, 2, 1)))
    nl = _DEBUG.get("nlayers", DEPTH) if _DEBUG.get("on") else DEPTH
    for l in range(nl):
        xT = _layer(l, xT, P)
        if _DEBUG.get("on"):
            _DEBUG[f"xT{l}"] = xT
    if nl == DEPTH:
        xT = _final_norm(xT, P["final_norm"])
    return np.ascontiguousarray(np.transpose(xT, (0, 2, 1))).astype(np.float32)
```
